# Optimizing a Trainium2 kernel written in Bass

```python
import math
import jax, jax.numpy as jnp
from jax import lax
import numpy as np

D_MODEL = 1024
BATCH = 16
SEQ = 2048
DEPTH = 2
DEC_BATCH = 8
DEC_SEQ = 2048
PAST_LEN = 128

GRID_W = 64
D_A = D_MODEL // 2
HEAD_A = 64
H_A = D_A // HEAD_A
LORA_W = 64
LORA_A = 64
GN_EPS = 6.4e-4
D_B = D_MODEL // 2
HEAD_B = 64
H_B = D_B // HEAD_B
MAX_KR = 8
WIN_C = 16
Q_BLK_W = 16
K_BLK_W = 32
N_CB = GRID_W // Q_BLK_W
RMS_EPS = 1e-6

A_SHIFT = 3 * D_A + 2 * LORA_W + 2 * LORA_A
OFF_GA = A_SHIFT
OFF_QB = OFF_GA + D_A
OFF_KB = OFF_QB + D_B
OFF_VB = OFF_KB + D_B
OFF_GB = OFF_VB + D_B
OFF_MA = OFF_GB + D_B
OFF_MB = OFF_MA + D_MODEL
D_IN = OFF_MB + D_MODEL

kernel_name = "hybrid_rwkv7_natten_encoder"


def rms_norm(x, g):
    xf = x.astype(jnp.float32)
    y = xf * lax.rsqrt(jnp.mean(xf * xf, axis=-1, keepdims=True) + RMS_EPS)
    return (y * g.astype(jnp.float32)).astype(x.dtype)


def centred_shift(z, mu):
    zp = jnp.pad(z[:, :-1], ((0, 0), (1, 0), (0, 0)))
    zn = jnp.pad(z[:, 1:], ((0, 0), (0, 1), (0, 0)))
    return z + mu * (0.5 * (zp + zn) - z)


def _dirs_shared(x):
    return jnp.stack([x, jnp.flip(x, 1)], 0).transpose(2, 0, 1, 3, 4)


def _dirs(x):
    return jnp.stack([x[:, :, 0], jnp.flip(x[:, :, 1], 1)], 0).transpose(2, 0, 1, 3, 4)


def _rwkv7_step(S, inp):
    r, w, k, v, aa, bb = inp
    sa = jnp.einsum('dbhvk,dbhk->dbhv', S, aa)
    S = S * w[..., None, :] + sa[..., None] * bb[..., None, :] + v[..., None] * k[..., None, :]
    o = jnp.einsum('dbhvk,dbhk->dbhv', S, r)
    return S, o


def rwkv7_bidir(r, k, v, wlo, alo, w0, w2, a0, a2, k_k, k_a, r_k, gn_w, gn_b):
    B, T, _ = r.shape
    f32 = jnp.float32
    r, k, v = r.astype(f32), k.astype(f32), v.astype(f32)
    w_raw = w0.astype(f32) + jnp.einsum('btdr,drc->btdc', jnp.tanh(wlo.astype(f32)), w2.astype(f32))
    log_w = -jax.nn.softplus(-w_raw) - 0.5
    decay = jnp.exp(-jnp.exp(log_w))
    a = jax.nn.sigmoid(a0.astype(f32) + jnp.einsum('btdr,drc->btdc', alo.astype(f32), a2.astype(f32)))
    kk = (k * k_k.astype(f32)).reshape(B, T, H_A, HEAD_A)
    kk = kk * lax.rsqrt(jnp.maximum(jnp.sum(kk * kk, -1, keepdims=True), 1e-24))
    k_dir = k[:, :, None, :] * (1.0 + (a - 1.0) * k_a.astype(f32))
    hs = (B, T, 2, H_A, HEAD_A)
    a = a.reshape(hs)
    k_dir = k_dir.reshape(hs)
    decay = decay.reshape(hs)
    rh = r.reshape(B, T, H_A, HEAD_A)
    vh = v.reshape(B, T, H_A, HEAD_A)
    bb = kk[:, :, None] * a
    xs = (_dirs_shared(rh), _dirs(decay), _dirs(k_dir), _dirs_shared(vh),
          _dirs_shared(-kk), _dirs(bb))
    S0 = jnp.zeros((2, B, H_A, HEAD_A, HEAD_A), f32)
    _, o = lax.scan(_rwkv7_step, S0, xs)
    o = o.transpose(1, 2, 0, 3, 4)
    o = o[0] + jnp.flip(o[1], 1)
    mu = jnp.mean(o, -1, keepdims=True)
    var = jnp.mean(jnp.square(o - mu), -1, keepdims=True)
    o = (o - mu) * lax.rsqrt(var + GN_EPS)
    o = o * gn_w.astype(f32).reshape(H_A, HEAD_A) + gn_b.astype(f32).reshape(H_A, HEAD_A)
    bonus = jnp.sum(rh[:, :, None] * k_dir * r_k.astype(f32), -1, keepdims=True) * vh[:, :, None]
    o = o + jnp.sum(bonus, axis=2)
    return o.reshape(B, T, D_A)


def neighbourhood_attention(q, k, v, rpb):
    B, T, _ = q.shape
    rows = T // GRID_W
    kr = min(MAX_KR, rows)
    g = (B, rows, GRID_W, H_B, HEAD_B)
    q = (q * (HEAD_B ** -0.5)).reshape(g)
    k = k.reshape(g)
    v = v.reshape(g)
    c0 = np.arange(N_CB) * Q_BLK_W
    kb = np.clip(c0 - WIN_C // 2, 0, GRID_W - K_BLK_W)
    key_cols = kb[:, None] + np.arange(K_BLK_W)
    q_cols = c0[:, None] + np.arange(Q_BLK_W)
    cs = np.clip(q_cols - WIN_C // 2, 0, GRID_W - WIN_C)
    kc = key_cols[:, None, :]
    col_mask = (kc >= cs[..., None]) & (kc < cs[..., None] + WIN_C)
    dc_idx = np.clip(kc - q_cols[..., None] + WIN_C - 1, 0, 2 * WIN_C - 2)
    rpb32 = rpb.astype(jnp.float32)

    def row_fn(r):
        rs = jnp.clip(r - kr // 2, 0, rows - kr)
        k_rows = lax.dynamic_slice_in_dim(k, rs, kr, axis=1)
        v_rows = lax.dynamic_slice_in_dim(v, rs, kr, axis=1)
        k_blk = k_rows[:, :, key_cols]
        v_blk = v_rows[:, :, key_cols]
        q_blk = lax.dynamic_index_in_dim(q, r, axis=1, keepdims=False).reshape(B, N_CB, Q_BLK_W, H_B, HEAD_B)
        s = jnp.einsum('bnqhd,binjhd->bhnqij', q_blk, k_blk, preferred_element_type=jnp.float32)
        dr_idx = rs + jnp.arange(kr) - r + (MAX_KR - 1)
        bias = rpb32[:, dr_idx][:, :, dc_idx].transpose(0, 2, 3, 1, 4)
        s = jnp.where(col_mask[None, None, :, :, None, :], s + bias[None], -1e30)
        p = jax.nn.softmax(s.reshape(B, H_B, N_CB, Q_BLK_W, kr * K_BLK_W), axis=-1)
        p = p.reshape(B, H_B, N_CB, Q_BLK_W, kr, K_BLK_W).astype(v.dtype)
        o = jnp.einsum('bhnqij,binjhd->bnqhd', p, v_blk)
        return o.reshape(B, GRID_W, H_B, HEAD_B)

    out = lax.map(row_fn, jnp.arange(rows))
    return jnp.moveaxis(out, 0, 1).reshape(B, T, D_B)


def hybrid_layer(x, norm_g, w_in, shift_mu, w0, w2, a0, a2, k_k, k_a, r_k, gn_w, gn_b,
                 rpb, w_pa, w_pb, w_out):
    B, T, _ = x.shape
    h = rms_norm(x, norm_g)
    z = jnp.einsum('btd,de->bte', h, w_in)
    za = centred_shift(z[..., :A_SHIFT], shift_mu)
    ra = za[..., 0:D_A]
    ka = za[..., D_A:2 * D_A]
    va = za[..., 2 * D_A:3 * D_A]
    wlo = za[..., 3 * D_A:3 * D_A + 2 * LORA_W].reshape(B, T, 2, LORA_W)
    alo = za[..., 3 * D_A + 2 * LORA_W:A_SHIFT].reshape(B, T, 2, LORA_A)
    ga = z[..., OFF_GA:OFF_QB]
    qb = z[..., OFF_QB:OFF_KB]
    kb = z[..., OFF_KB:OFF_VB]
    vb = z[..., OFF_VB:OFF_GB]
    gb = z[..., OFF_GB:OFF_MA]
    ma = z[..., OFF_MA:OFF_MB]
    mb = z[..., OFF_MB:D_IN]
    ya = rwkv7_bidir(ra, ka, va, wlo, alo, w0, w2, a0, a2, k_k, k_a, r_k, gn_w, gn_b)
    ya = (ya * jax.nn.silu(ga.astype(jnp.float32))).astype(x.dtype)
    yb = neighbourhood_attention(qb, kb, vb, rpb) * jax.nn.silu(gb)
    merged = (jax.nn.sigmoid(ma) * jnp.einsum('btc,cd->btd', ya, w_pa)
              + jax.nn.sigmoid(mb) * jnp.einsum('btc,cd->btd', yb, w_pb))
    return x + jnp.einsum('btd,de->bte', merged, w_out)


def trunk(x, norm_g, w_in, shift_mu, w0, w2, a0, a2, k_k, k_a, r_k, gn_w, gn_b,
          rpb, w_pa, w_pb, w_out, final_g):
    for l in range(DEPTH):
        x = hybrid_layer(x, norm_g[l], w_in[l], shift_mu[l], w0[l], w2[l], a0[l], a2[l],
                         k_k[l], k_a[l], r_k[l], gn_w[l], gn_b[l], rpb[l],
                         w_pa[l], w_pb[l], w_out[l])
    return rms_norm(x, final_g)


def setup_inputs(seed: int = 0) -> dict:
    key = jax.random.key(seed)
    ks = jax.random.split(key, 20)
    f32 = jnp.float32
    nrm = lambda k, s: jax.random.normal(k, s, f32)
    L = DEPTH
    return {
        'x_prompt': nrm(ks[0], (BATCH, SEQ, D_MODEL)),
        'x_sample': nrm(ks[1], (DEC_BATCH, DEC_SEQ, D_MODEL)),
        'norm_g': 1.0 + 0.02 * nrm(ks[2], (L, D_MODEL)),
        'w_in': nrm(ks[3], (L, D_MODEL, D_IN)) * D_MODEL ** -0.5,
        'shift_mu': jax.random.uniform(ks[4], (L, A_SHIFT), f32),
        'w0': jax.random.uniform(ks[5], (L, 2, D_A), f32, -6.0, 1.0),
        'w2': nrm(ks[6], (L, 2, LORA_W, D_A)) * (0.3 * LORA_W ** -0.5),
        'a0': 0.5 * nrm(ks[7], (L, 2, D_A)),
        'a2': nrm(ks[8], (L, 2, LORA_A, D_A)) * (0.3 * LORA_A ** -0.5),
        'k_k': 0.85 + 0.05 * nrm(ks[9], (L, D_A)),
        'k_a': 1.0 + 0.05 * nrm(ks[10], (L, D_A)),
        'r_k': 0.1 * nrm(ks[11], (L, H_A, HEAD_A)),
        'gn_w': 1.0 + 0.02 * nrm(ks[12], (L, D_A)),
        'gn_b': 0.02 * nrm(ks[13], (L, D_A)),
        'rpb': 0.1 * nrm(ks[14], (L, H_B, 2 * MAX_KR - 1, 2 * WIN_C - 1)),
        'w_pa': nrm(ks[15], (L, D_A, D_MODEL)) * D_A ** -0.5,
        'w_pb': nrm(ks[16], (L, D_B, D_MODEL)) * D_B ** -0.5,
        'w_out': nrm(ks[17], (L, D_MODEL, D_MODEL)) * D_MODEL ** -0.5,
        'final_g': 1.0 + 0.02 * nrm(ks[18], (D_MODEL,)),
    }


def reference(x_prompt, x_sample, norm_g, w_in, shift_mu, w0, w2, a0, a2, k_k, k_a, r_k,
              gn_w, gn_b, rpb, w_pa, w_pb, w_out, final_g):
    y_prompt = trunk(x_prompt, norm_g, w_in, shift_mu, w0, w2, a0, a2, k_k, k_a, r_k,
                     gn_w, gn_b, rpb, w_pa, w_pb, w_out, final_g)
    y_sample = trunk(x_sample, norm_g, w_in, shift_mu, w0, w2, a0, a2, k_k, k_a, r_k,
                     gn_w, gn_b, rpb, w_pa, w_pb, w_out, final_g)
    return (y_prompt, y_sample)
```

```python
import numpy as np
import concourse.bass as bass
import concourse.mybir as mybir
from concourse.bass_utils import run_bass_kernel_spmd

F32 = mybir.dt.float32
BF16 = mybir.dt.bfloat16
AF = mybir.ActivationFunctionType
ALU = mybir.AluOpType
AX = mybir.AxisListType

D = 1024
DIN = 6400
L = 2
NCORES = 8
OFF_GA = 1792
OFF_QB = OFF_GA + 512
OFF_KB = OFF_QB + 512
OFF_VB = OFF_KB + 512
OFF_GB = OFF_VB + 512
OFF_MA = OFF_GB + 512
OFF_MB = OFF_MA + 1024
RMS_EPS = 1e-6
GN_EPS = 6.4e-4
CDEC = float(np.exp(-0.5))
NE = 19
PC_PER_L = 50


class Tok:
    __slots__ = ("w", "r", "name")

    def __init__(self, name=""):
        self.w = None
        self.r = {}
        self.name = name


class Prog:
    ENGS = ("pe", "act", "dve", "pool", "sp")

    def __init__(self):
        self.ops = {e: [] for e in self.ENGS}
        self.cnt = {e: 0 for e in ("pe", "act", "dve", "pool")}
        self.seen = {e: {} for e in self.ENGS}
        self.dcnt = {}
        self.pending = {e: {} for e in self.ENGS}

    def _need(self, eng, waits, key, val, raw, is_dma=False):
        if key == eng and not is_dma and eng == "pe":
            return
        if self.seen[eng].get(key, 0) >= val:
            return
        if waits.get(key, 0) < val:
            waits[key] = val

    def _deps(self, eng, reads, writes, is_dma=False):
        waits = dict(self.pending[eng])
        self.pending[eng] = {}
        for t in reads:
            if t.w is not None:
                self._need(eng, waits, t.w[0], t.w[1], True, is_dma)
        for t in writes:
            if t.w is not None:
                self._need(eng, waits, t.w[0], t.w[1], False, is_dma)
            for k, v in t.r.items():
                self._need(eng, waits, k, v, False, is_dma)
        for k, v in waits.items():
            if self.seen[eng].get(k, 0) < v:
                self.seen[eng][k] = v
        return waits

    def op(self, eng, fn, reads=(), writes=()):
        waits = self._deps(eng, reads, writes)
        self.cnt[eng] += 1
        c = self.cnt[eng]
        for t in reads:
            if t.r.get(eng, 0) < c:
                t.r[eng] = c
        for t in writes:
            t.w = (eng, c)
            t.r = {}
        self.ops[eng].append((waits, fn, eng, 1))

    def dma(self, queue, dsem, fn, reads=(), writes=()):
        waits = self._deps(queue, reads, writes, True)
        self.dcnt[dsem] = self.dcnt.get(dsem, 0) + 16
        c = self.dcnt[dsem]
        for t in reads:
            if t.r.get(dsem, 0) < c:
                t.r[dsem] = c
        for t in writes:
            t.w = (dsem, c)
            t.r = {}
        self.ops[queue].append((waits, fn, dsem, 16))

    def barrier(self):
        snap = dict(self.cnt)
        snap.update(self.dcnt)
        for e in self.ENGS:
            for k, v in snap.items():
                if k == e:
                    continue
                if self.seen[e].get(k, 0) < v and self.pending[e].get(k, 0) < v:
                    self.pending[e][k] = v

    def final_wait(self, eng="sp"):
        waits = {}
        for k, v in self.dcnt.items():
            waits[k] = v
        for k, v in self.cnt.items():
            waits[k] = v
        self.ops[eng].append((waits, None, None, 0))


def build_program(T, NSEQ):
    assert T % 512 == 0
    NT = T // 128
    NST = T // 256
    ROWS = T // 64
    nc = bass.Bass("TRN2", target_bir_lowering=False)
    P = Prog()

    x_d = nc.dram_tensor("x", [NSEQ, T, D], F32, kind="ExternalInput").ap()
    y_d = nc.dram_tensor("y", [NSEQ, T, D], F32, kind="ExternalOutput").ap()
    x1_d = nc.dram_tensor("x1s", [NSEQ, T, D], F32, kind="ExternalOutput").ap()
    win_d = nc.dram_tensor("w_in", [L, D, DIN], F32, kind="ExternalInput").ap()
    wpa_d = nc.dram_tensor("w_pa", [L, 512, D], F32, kind="ExternalInput").ap()
    wpb_d = nc.dram_tensor("w_pb", [L, 512, D], F32, kind="ExternalInput").ap()
    wout_d = nc.dram_tensor("w_out", [L, D, D], F32, kind="ExternalInput").ap()
    g_d = nc.dram_tensor("gains", [3, D], F32, kind="ExternalInput").ap()
    pcol_d = nc.dram_tensor("pcol", [128, L * PC_PER_L], F32, kind="ExternalInput").ap()
    w2_d = nc.dram_tensor("w2t", [L, 128, 512], F32, kind="ExternalInput").ap()
    a2_d = nc.dram_tensor("a2t", [L, 128, 512], F32, kind="ExternalInput").ap()
    cst_d = nc.dram_tensor("consts", [128, 1280], F32, kind="ExternalInput").ap()
    emask_d = nc.dram_tensor("emask", [128, NE * 128], F32, kind="ExternalInput").ap()
    rpb_d = nc.dram_tensor("rpbT", [L, 4, 128, NE * 128], F32, kind="ExternalInput").ap()

    class Bump:
        def __init__(self, start):
            self.off = start

        def alloc(self, name, shape, dt):
            nbytes = int(np.prod(shape[1:])) * (4 if dt == F32 else 2)
            nbytes = (nbytes + 31) // 32 * 32
            t = nc.alloc_sbuf_tensor_at(name, list(shape), dt, offset=self.off)
            self.off += nbytes
            return t

    pers = Bump(17408)
    hT = pers.alloc("hT", [128, 8, T + 2], BF16)
    yaT = pers.alloc("yaT", [128, 4, T], BF16)
    ybT = pers.alloc("ybT", [128, 4, T], BF16)
    NW = 5
    wslot = [pers.alloc(f"ws{i}", [128, 8, 128], BF16) for i in range(NW)]
    cmask = pers.alloc("cmask", [128, 4, 128], BF16)
    ident_b = pers.alloc("ident_b", [128, 128], BF16)
    ident_f = pers.alloc("ident_f", [128, 128], F32)
    bones_f = pers.alloc("bones_f", [128, 128], F32)
    ones_b = pers.alloc("ones_b", [128, 128], BF16)
    rmask = pers.alloc("rmask", [128, 256], F32)
    pcol = pers.alloc("pcol", [128, L * PC_PER_L], F32)
    pc2 = pers.alloc("pc2", [128, L * 28], F32)
    _w2 = [pers.alloc(f"w2t{d_}", [128, 512], BF16) for d_ in range(2)]
    _a2 = [pers.alloc(f"a2t{d_}", [128, 512], BF16) for d_ in range(2)]
    w2t = [_w2 for l in range(L)]
    a2t = [_a2 for l in range(L)]
    PH0 = pers.off

    phN = Bump(PH0)
    mergedT = phN.alloc("mergedT", [128, 8, T], BF16)
    woutT = phN.alloc("woutT", [128, 8, D], BF16)
    xst = [phN.alloc(f"xst{i}", [128, D], F32) for i in range(2)]
    xnew = [phN.alloc(f"xnew{i}", [128, D], F32) for i in range(2)]
    yst = [phN.alloc(f"yst{i}", [128, D], F32) for i in range(2)]
    junk = phN.alloc("junk", [128, D], BF16)
    htok = [phN.alloc(f"htok{i}", [128, D], BF16) for i in range(2)]
    stat = phN.alloc("stat", [128, 16], F32)
    msg = [phN.alloc("msg0", [128, 4, 512], F32)] * 2
    grep = [phN.alloc(f"grep{i}", [128, D], F32) for i in range(2)]
    cst_f = phN.alloc("cst_f", [128, 1280], F32)
    endN = phN.off

    phH = Bump(PH0)
    loraW = phH.alloc("loraW", [128, T], BF16)
    loraA = phH.alloc("loraA", [128, T], BF16)
    sga = phH.alloc("sga", [128, T], BF16)
    zs = [phH.alloc(f"zs{i}", [128, 258], F32) for i in range(2)]
    HP0 = phH.off
    phA = Bump(HP0)
    qT = phA.alloc("qT", [128, T // 64, 128], BF16)
    kT = phA.alloc("kT", [128, T], BF16)
    sgb = phA.alloc("sgb", [128, T], BF16)
    vbt = phA.alloc("vbt", [128, NT, 128], BF16)
    rpbb = phA.alloc("rpbb", [128, NE, 128], BF16)
    etf = phA.alloc("etf", [128, NE, 128], F32)
    etab = phA.alloc("etab", [128, NE, 128], BF16)
    emask = phA.alloc("emask", [128, NE, 128], BF16)
    exb = [phA.alloc(f"exb{i}", [128, 5, 128], BF16) for i in range(3)]
    ptb = [phA.alloc(f"ptb{i}", [128, 5, 128], BF16) for i in range(3)]
    rden = [phA.alloc(f"rden{i}", [128, 64], F32) for i in range(3)]
    g2 = [phA.alloc(f"g2{i}", [128, 64], F32) for i in range(3)]
    endA = phA.off
    phR = Bump(HP0)
    o0T = phR.alloc("o0T", [128, T], F32)
    bon0 = phR.alloc("bon0", [128, T], BF16)
    FN = ("rs", "ks", "vs", "kkn", "sig", "aa", "fc", "uu", "ww", "inc", "e1", "e2", "e3", "e4", "kdir")
    TB = []
    for d_ in range(2):
        B = {}
        for nm in FN:
            B[nm] = phR.alloc(f"f{d_}_" + nm, [128, 256], F32)
        B["wc"] = [phR.alloc(f"wc{d_}{q}", [128, 4], F32) for q in range(2)]
        B["bon"] = [phR.alloc(f"bon{d_}{q}", [128, 256], BF16) for q in range(2)]
        B["AR"] = [phR.alloc(f"AR{d_}{q}", [128, 4, 2, 128], BF16) for q in range(2)]
        for nm in ("KT", "BT", "KH", "BH", "VT"):
            B[nm] = [phR.alloc(f"{nm}{d_}{q}", [128, 4, 128], BF16) for q in range(2)]
        for nm in ("Atok", "KHtok", "Vtok", "AakT", "ArkT"):
            B[nm] = phR.alloc(f"{nm}{d_}", [128, 4, 128], BF16)
        B["RB"] = phR.alloc(f"RB{d_}", [128, 4, 2, 128], BF16)
        B["QP"] = [phR.alloc(f"QP{d_}{i}", [128, 4, 2, 128], BF16) for i in range(2)]
        B["Qb"] = [phR.alloc(f"Qb{d_}{i}", [128, 4, 128], BF16) for i in range(2)]
        B["GT"] = phR.alloc(f"GT{d_}", [128, 4, 128], F32)
        B["ObT"] = phR.alloc(f"ObT{d_}", [128, 256], F32)
        B["Hs"] = phR.alloc(f"Hs{d_}", [128, 4, 128], F32)
        B["Sst"] = phR.alloc(f"Sst{d_}", [128, 128], F32)
        B["Sbf"] = phR.alloc(f"Sbf{d_}", [128, 128], BF16)
        TB.append(B)
    endR = phR.off
    top = max(endN, endA, endR)
    assert top <= 229376, (PH0, endN, endA, endR)

    banks = [nc.alloc_psum_tensor(f"pb{i}", [128, 512], F32) for i in range(8)]
    bank_tok = [Tok(f"bank{i}") for i in range(8)]
    rr = [0]

    def bk(i):
        return banks[i][:, :]

    def nextbank(n=6):
        i = rr[0] % n
        rr[0] += 1
        return i

    tk = {}

    def T_(name):
        if name not in tk:
            tk[name] = Tok(name)
        return tk[name]

    def mm(out, lhsT, rhs, start, stop, reads, writes):
        P.op("pe", lambda e, o=out, l=lhsT, r=rhs, s=start, t=stop: e.matmul(o, lhsT=l, rhs=r, start=s, stop=t),
             reads, writes)

    def tr(out, in_, reads, writes):
        P.op("pe", lambda e, o=out, i=in_: e.transpose(out=o, in_=i, identity=ident_b[:, :]),
             reads + [T_("ident_b")], writes)

    def act(out, in_, func, reads, writes, bias=None, scale=None):
        kw = {}
        if bias is not None:
            kw["bias"] = bias
        if scale is not None:
            kw["scale"] = scale
        P.op("act", lambda e, o=out, i=in_, f=func, kw=kw: e.activation(out=o, in_=i, func=f, **kw), reads, writes)

    def tt(eng, out, in0, in1, op, reads, writes):
        P.op(eng, lambda e, o=out, a=in0, b=in1, p=op: e.tensor_tensor(out=o, in0=a, in1=b, op=p), reads, writes)

    def ts(eng, out, in0, s1, s2, op0, op1, reads, writes):
        if op1 is None:
            P.op(eng, lambda e, o=out, a=in0, s=s1, p=op0: e.tensor_scalar(out=o, in0=a, scalar1=s, scalar2=None, op0=p),
                 reads, writes)
        else:
            P.op(eng, lambda e, o=out, a=in0, s=s1, q=s2, p=op0, r=op1:
                 e.tensor_scalar(out=o, in0=a, scalar1=s, scalar2=q, op0=p, op1=r), reads, writes)

    def stt(out, in0, scalar, in1, op0, op1, reads, writes):
        P.op("dve", lambda e, o=out, a=in0, s=scalar, b=in1, p=op0, q=op1:
             e.scalar_tensor_tensor(out=o, in0=a, scalar=s, in1=b, op0=p, op1=q), reads, writes)

    def cp(eng, out, in_, reads, writes):
        if eng == "act":
            P.op("act", lambda e, o=out, i=in_: e.activation(out=o, in_=i, func=AF.Copy), reads, writes)
        else:
            P.op(eng, lambda e, o=out, i=in_: e.tensor_copy(out=o, in_=i), reads, writes)

    def recip(out, in_, reads, writes):
        P.op("dve", lambda e, o=out, i=in_: e.reciprocal(out=o, in_=i), reads, writes)

    def dma(queue, dsem, out, in_, reads, writes):
        P.dma(queue, dsem, lambda e, o=out, i=in_: e.dma_start(out=o, in_=i), reads, writes)

    t_c = T_("consts")
    dma("sp", "d_c0", cst_f[:, :], cst_d[:, :], [], [T_("cst_f")])
    dma("sp", "d_c1", pcol[:, :], pcol_d[:, :], [], [T_("pcol")])
    for d_ in range(2):
        P.op("pool", lambda e, t=_w2[d_]: e.memset(t[:, :], 0.0), [], [T_("w2t")])
        P.op("pool", lambda e, t=_a2[d_]: e.memset(t[:, :], 0.0), [], [T_("w2t")])

    def load_lora2(l):
        for d_ in range(2):
            hs_ = slice(d_ * 64, (d_ + 1) * 64)
            dma("pool", "d_c3", _w2[d_][hs_, :], w2_d[l][hs_, :], [], [T_("w2t")])
            dma("pool", "d_c3", _a2[d_][hs_, :], a2_d[l][hs_, :], [], [T_("w2t")])

    cp("dve", cmask[:, :, :], cst_f[:, 0:512].rearrange("p (a b) -> p a b", b=128), [T_("cst_f")], [T_("cmask")])
    cp("dve", ident_b[:, :], cst_f[:, 512:640], [T_("cst_f")], [T_("ident_b")])
    cp("dve", ident_f[:, :], cst_f[:, 512:640], [T_("cst_f")], [T_("ident_f")])
    cp("dve", bones_f[:, :], cst_f[:, 640:768], [T_("cst_f")], [T_("bones_f")])
    cp("dve", rmask[:, :], cst_f[:, 768:1024], [T_("cst_f")], [T_("rmask")])
    cp("dve", ones_b[:, :], cst_f[:, 1024:1152], [T_("cst_f")], [T_("ones_b")])
    for l in range(L):
        ts("dve", pc2[:, l * 28:l * 28 + 14], pcol[:, l * PC_PER_L:l * PC_PER_L + 14], -1.0, 1.0, ALU.mult, ALU.add,
           [T_("pcol")], [T_("pc2")])
        ts("dve", pc2[:, l * 28 + 14:l * 28 + 28], pcol[:, l * PC_PER_L:l * PC_PER_L + 14], 0.5, None, ALU.mult, None,
           [T_("pcol")], [T_("pc2")])
    P.op("pool", lambda e: e.memset(hT[:, :, 0:1], 0.0), [], [T_("hTpad")])
    P.op("pool", lambda e: e.memset(hT[:, :, T + 1:T + 2], 0.0), [], [T_("hTpad")])

    def pc(l, name, idx=0):
        base = l * PC_PER_L
        offs = {"mu": 0, "w0": 14, "a0": 22, "k_k": 30, "k_a": 34, "r_k": 38, "gn_w": 42, "gn_b": 46}
        c = base + offs[name] + idx
        return pcol[:, c:c + 1]

    wtok = [Tok(f"ws{i}") for i in range(NW)]
    wrr = [0]

    def load_w(src_ap, kcs):
        i = wrr[0] % NW
        wrr[0] += 1
        dma("pool", f"d_w{i}", wslot[i][:, 0:kcs, :], src_ap.rearrange("(k p) c -> p k c", p=128), [], [wtok[i]])
        return i

    def load_g(which, slot):
        dma("sp", f"d_g{slot}", grep[slot][:, :], g_d[which, :].partition_broadcast(128), [], [T_(f"grep{slot}")])

    def norm_tile(src, src_tok, i, gslot, eps, out_h=None, out_f=None, out_tok=None):
        k = i % 2
        if i == 2:
            chk(109)
        chk(100)
        act(junk[:, :], src, AF.Square, [src_tok], [T_("junk")])
        chk(101)
        P.op("dve", lambda e, i=i: e.tensor_reduce(out=stat[:, i % 16:i % 16 + 1], in_=junk[:, :], axis=AX.X, op=ALU.add),
             [T_("junk")], [T_(f"stat{i % 16}")])
        chk(102)
        act(stat[:, i % 16:i % 16 + 1], stat[:, i % 16:i % 16 + 1], AF.Sqrt, [T_(f"stat{i % 16}")], [T_(f"stat{i % 16}")],
            bias=float(eps), scale=1.0 / D)
        chk(103)
        recip(stat[:, i % 16:i % 16 + 1], stat[:, i % 16:i % 16 + 1], [T_(f"stat{i % 16}")], [T_(f"stat{i % 16}")])
        chk(104)
        if out_f is not None:
            stt(out_f, src, stat[:, i % 16:i % 16 + 1], grep[gslot][:, :], ALU.mult, ALU.mult,
                [src_tok, T_(f"stat{i % 16}"), T_(f"grep{gslot}")], [out_tok])
            return
        stt(htok[k][:, :], src, stat[:, i % 16:i % 16 + 1], grep[gslot][:, :], ALU.mult, ALU.mult,
            [src_tok, T_(f"stat{i % 16}"), T_(f"grep{gslot}")], [T_(f"htok{k}")])
        chk(105)
        b = nextbank()
        pb = bk(b).bitcast(BF16)
        for kc in range(8):
            tr(pb[:, kc * 128:(kc + 1) * 128], htok[k][:, kc * 128:(kc + 1) * 128], [T_(f"htok{k}")], [bank_tok[b]])
        chk(106)
        cp("act", hT[:, :, 1 + i * 128:1 + (i + 1) * 128], pb[:, 0:1024].rearrange("p (k t) -> p k t", t=128),
           [bank_tok[b]], [T_("hT")])
        chk(107)
        if i == 1:
            chk(108)
        if i == 2:
            chk(110)
        if i == 3:
            chk(111)

    def zmm(b, ncols, wi, kcs, rhs_fn, rhs_tok, col0=0, extra=()):
        for kc in range(kcs):
            mm(bk(b)[:, col0:col0 + ncols], wslot[wi][:, kc, :], rhs_fn(kc), kc == 0, kc == kcs - 1,
               [wtok[wi], rhs_tok] + list(extra), [bank_tok[b]])

    def shifted_block(l, wi, mublk, t0, out_ap, out_tok, k):
        b = nextbank()
        zmm(b, 258, wi, 8, lambda kc: hT[:, kc, t0:t0 + 258], T_("hT"), extra=[T_("hTpad")])
        cp("act", zs[k][:, :], bk(b)[:, 0:258], [bank_tok[b]], [T_(f"zs{k}")])
        tt("dve", TB[k]["uu"][:, :], zs[k][:, 0:256], zs[k][:, 2:258], ALU.add, [T_(f"zs{k}")], [T_(f"f{k}_uu")])
        ts("dve", TB[k]["uu"][:, :], TB[k]["uu"][:, :], pc2[:, l * 28 + 14 + mublk:l * 28 + 15 + mublk], None, ALU.mult, None,
           [T_(f"f{k}_uu"), T_("pc2")], [T_(f"f{k}_uu")])
        stt(out_ap, zs[k][:, 1:257], pc2[:, l * 28 + mublk:l * 28 + mublk + 1], TB[k]["uu"][:, :], ALU.mult, ALU.add,
            [T_(f"zs{k}"), T_(f"f{k}_uu"), T_("pc2")], [out_tok])

    HB = [slice(0, 64), slice(64, 128)]
    import os as _os
    KSTOP = int(_os.environ.get("KSTOP", "99"))

    class _Stop(Exception):
        pass

    def chk(stage):
        if stage == KSTOP:
            raise _Stop()

    def body():

        for s in range(NSEQ):
            for l in range(L):
                if l == 0:
                    P.barrier()
                    load_g(0, 0)
                    for i in range(NT):
                        k = i % 2
                        dma("sp", f"d_x{k}", xst[k][:, :], x_d[s, i * 128:(i + 1) * 128, :], [], [T_(f"xst{k}")])
                        norm_tile(xst[k][:, :], T_(f"xst{k}"), i, 0, RMS_EPS)
                P.barrier()
                w_l = win_d[l]
                load_lora2(l)
                for j in range(2):
                    wi = load_w(w_l[:, 1536 + j * 128:1536 + (j + 1) * 128], 8)
                    for st in range(NST):
                        k = st % 2
                        shifted_block(l, wi, 12 + j, st * 256, TB[k]["e1"][:, :], T_(f"f{k}_e1"), k)
                        if j == 0:
                            act(loraW[:, st * 256:(st + 1) * 256], TB[k]["e1"][:, :], AF.Tanh, [T_(f"f{k}_e1")], [T_("loraW")])
                        else:
                            cp("act", loraA[:, st * 256:(st + 1) * 256], TB[k]["e1"][:, :], [T_(f"f{k}_e1")], [T_("loraA")])
                chk(2)
                for hp in range(4):
                    c0 = hp * 128
                    P.barrier()
                    dma("pool", "d_c2", emask[:, :, :], emask_d.rearrange("p (e c) -> p e c", c=128), [], [T_("emask")])
                    dma("pool", "d_rpb", rpbb[:, :, :], rpb_d[l, hp].rearrange("p (e c) -> p e c", c=128), [], [T_("rpbb")])
                    act(etf[:, :, :], rpbb[:, :, :], AF.Exp, [T_("rpbb")], [T_("etf")])
                    tt("dve", etab[:, :, :], etf[:, :, :], emask[:, :, :], ALU.mult, [T_("etf"), T_("emask")], [T_("etab")])
                    P.op("pool", lambda e: e.memset(qT[:, :, :], 0.0), [], [T_("qT")])
                    wq = load_w(w_l[:, OFF_QB + c0:OFF_QB + c0 + 128], 8)
                    wk = load_w(w_l[:, OFF_KB + c0:OFF_KB + c0 + 128], 8)
                    wv = load_w(w_l[:, OFF_VB + c0:OFF_VB + c0 + 128], 8)
                    wgb = load_w(w_l[:, OFF_GB + c0:OFF_GB + c0 + 128], 8)
                    wga = load_w(w_l[:, OFF_GA + c0:OFF_GA + c0 + 128], 8)
                    for t4 in range(T // 512):
                        tsl = slice(t4 * 512, (t4 + 1) * 512)
                        hsl = slice(1 + t4 * 512, 1 + (t4 + 1) * 512)
                        b = nextbank()
                        zmm(b, 512, wq, 8, lambda kc, hsl=hsl: hT[:, kc, hsl], T_("hT"))
                        for h in range(2):
                            act(qT[HB[h], t4 * 8:(t4 + 1) * 8, h * 64:(h + 1) * 64],
                                bk(b)[HB[h], :].rearrange("p (r q) -> p r q", q=64), AF.Copy,
                                [bank_tok[b]], [T_("qT")], scale=0.125)
                        b = nextbank()
                        zmm(b, 512, wk, 8, lambda kc, hsl=hsl: hT[:, kc, hsl], T_("hT"))
                        cp("dve", kT[:, tsl], bk(b)[:, :], [bank_tok[b]], [T_("kT")])
                        b = nextbank()
                        zmm(b, 512, wgb, 8, lambda kc, hsl=hsl: hT[:, kc, hsl], T_("hT"))
                        act(sgb[:, tsl], bk(b)[:, :], AF.Silu, [bank_tok[b]], [T_("sgb")])
                        b = nextbank()
                        zmm(b, 512, wga, 8, lambda kc, hsl=hsl: hT[:, kc, hsl], T_("hT"))
                        act(sga[:, tsl], bk(b)[:, :], AF.Silu, [bank_tok[b]], [T_("sga")])
                        b = nextbank()
                        for q4 in range(4):
                            i = t4 * 4 + q4
                            for kc in range(8):
                                mm(bk(b)[:, q4 * 128:(q4 + 1) * 128], hT[:, kc, 1 + i * 128:1 + (i + 1) * 128],
                                   wslot[wv][:, kc, :], kc == 0, kc == 7, [T_("hT"), wtok[wv]], [bank_tok[b]])
                        cp("dve", vbt[:, t4 * 4:(t4 + 1) * 4, :], bk(b)[:, :].rearrange("p (a c) -> p a c", c=128),
                           [bank_tok[b]], [T_("vbt")])
                    chk(3)
                    NB_ROW = 3

                    def row_front(r):
                        k = r % NB_ROW
                        rs_ = min(max(r - 4, 0), ROWS - 8)
                        if rs_ % 2 == 0:
                            nt_, tile0, e0, estep = 4, rs_ // 2, rs_ - r + 7, 2
                        else:
                            nt_, tile0, e0, estep = 5, (rs_ - 1) // 2, 14, 1
                        bA = nextbank()
                        bB = nextbank() if nt_ == 5 else None
                        for i in range(nt_):
                            bb_, col = (bA, i * 128) if i < 4 else (bB, 0)
                            mm(bk(bb_)[:, col:col + 128], kT[:, (tile0 + i) * 128:(tile0 + i + 1) * 128], qT[:, r, :],
                               True, True, [T_("kT"), T_("qT")], [bank_tok[bb_]])
                        act(exb[k][:, 0:4, :], bk(bA)[:, :].rearrange("p (a c) -> p a c", c=128), AF.Exp,
                            [bank_tok[bA]], [T_(f"exb{k}")])
                        if nt_ == 5:
                            act(exb[k][:, 4, :], bk(bB)[:, 0:128], AF.Exp, [bank_tok[bB]], [T_(f"exb{k}")])
                        tt("dve", ptb[k][:, 0:nt_, :], exb[k][:, 0:nt_, :], etab[:, e0:e0 + estep * (nt_ - 1) + 1:estep, :],
                           ALU.mult, [T_(f"exb{k}"), T_("etab")], [T_(f"ptb{k}")])
                        return nt_, tile0

                    def row_back(r, nt_, tile0):
                        k = r % NB_ROW
                        bO = nextbank()
                        for i in range(nt_):
                            mm(bk(bO)[:, 0:128], vbt[:, tile0 + i, :], ptb[k][:, i, :], i == 0, i == nt_ - 1,
                               [T_("vbt"), T_(f"ptb{k}")], [bank_tok[bO]])
                        for i in range(nt_):
                            mm(bk(bO)[:, 128:256], ones_b[:, :], ptb[k][:, i, :], i == 0, i == nt_ - 1,
                               [T_("ones_b"), T_(f"ptb{k}")], [bank_tok[bO]])
                        for h in range(2):
                            recip(rden[k][HB[h], :], bk(bO)[HB[h], 128 + h * 64:128 + (h + 1) * 64],
                                  [bank_tok[bO]], [T_(f"rden{k}")])
                        tt("dve", g2[k][:, :], rden[k][:, :], sgb[:, r * 64:(r + 1) * 64], ALU.mult,
                           [T_(f"rden{k}"), T_("sgb")], [T_(f"g2{k}")])
                        for h in range(2):
                            tt("dve", ybT[HB[h], hp, r * 64:(r + 1) * 64], bk(bO)[HB[h], h * 64:(h + 1) * 64],
                               g2[k][HB[h], :], ALU.mult, [bank_tok[bO], T_(f"g2{k}")], [T_("ybT")])

                    LOOK = 2
                    fr = {}
                    for r in range(min(LOOK, ROWS)):
                        fr[r] = row_front(r)
                    for r in range(ROWS):
                        if r + LOOK < ROWS:
                            fr[r + LOOK] = row_front(r + LOOK)
                        row_back(r, *fr.pop(r))
                    chk(4)
                    P.barrier()
                    wr = load_w(w_l[:, c0:c0 + 128], 8)
                    wkk = load_w(w_l[:, 512 + c0:512 + c0 + 128], 8)
                    wvv = load_w(w_l[:, 1024 + c0:1024 + c0 + 128], 8)

                    def v3(ap):
                        return ap.rearrange("p (c k) -> p c k", k=64)

                    def pb3(b, lo=0, hi=512):
                        return bk(b)[:, lo:hi].rearrange("p (c k) -> p c k", k=128)

                    SY = {"prep": [0, 0], "main": [0, 0], "out": set()}

                    def prep_gen(d, l=l, hp=hp, c0=c0, wr=wr, wkk=wkk, wvv=wvv):
                        B = TB[d]

                        def f(n):
                            return B[n][:, :]

                        def ft(n):
                            return T_(f"f{d}_{n}")

                        for q in range(2):
                            for nm in ("AR", "KT", "BT", "KH", "BH", "VT"):
                                buf = B[nm][q]
                                P.op("pool", lambda e, buf=buf: e.memset(buf[tuple(slice(None) for _ in buf.shape)], 0.0), [],
                                     [T_(f"{nm}{q}_{d}")])
                        st_order = list(range(NST)) if d == 0 else list(range(NST - 1, -1, -1))
                        for idx, st in enumerate(st_order):
                            q = idx % 2
                            while SY["main"][d] < idx - 1:
                                yield 0
                            AR, KT, BT, KH, BH, VT, wc = B["AR"][q], B["KT"][q], B["BT"][q], B["KH"][q], B["BH"][q], B["VT"][q], B["wc"][q]

                            def X(n):
                                return T_(f"{n}{q}_{d}")

                            first = (st < NST // 2) if d == 0 else (st >= NST // 2)
                            t0 = st * 256
                            tsl = slice(t0, t0 + 256)
                            shifted_block(l, wr, hp, t0, f("rs"), ft("rs"), d)
                            yield 0
                            shifted_block(l, wkk, 4 + hp, t0, f("ks"), ft("ks"), d)
                            yield 0
                            shifted_block(l, wvv, 8 + hp, t0, f("vs"), ft("vs"), d)
                            yield 0
                            ts("dve", f("uu"), f("ks"), pc(l, "k_k", hp), None, ALU.mult, None, [ft("ks"), T_("pcol")], [ft("uu")])
                            tt("dve", f("ww"), f("uu"), f("uu"), ALU.mult, [ft("uu")], [ft("ww")])
                            b = nextbank(5)
                            mm(bk(b)[:, 0:256], bones_f[:, :], f("ww"), True, True, [T_("bones_f"), ft("ww")], [bank_tok[b]])
                            b2_ = nextbank(5)
                            mm(bk(b2_)[:, 0:256], w2t[l][d][:, c0:c0 + 128], loraW[:, tsl], True, True,
                               [T_("w2t"), T_("loraW")], [bank_tok[b2_]])
                            mm(bk(b2_)[:, 256:512], a2t[l][d][:, c0:c0 + 128], loraA[:, tsl], True, True,
                               [T_("w2t"), T_("loraA")], [bank_tok[b2_]])
                            act(f("sig"), bk(b2_)[:, 0:256], AF.Sigmoid, [bank_tok[b2_], T_("pcol")], [ft("sig")],
                                bias=pc(l, "w0", d * 4 + hp))
                            act(f("aa"), bk(b2_)[:, 256:512], AF.Sigmoid, [bank_tok[b2_], T_("pcol")], [ft("aa")],
                                bias=pc(l, "a0", d * 4 + hp))
                            act(f("e1"), bk(b)[:, 0:256], AF.Ln, [bank_tok[b]], [ft("e1")], bias=1e-24)
                            act(f("e1"), f("e1"), AF.Exp, [ft("e1")], [ft("e1")], scale=-0.5)
                            yield 0
                            tt("dve", f("kkn"), f("uu"), f("e1"), ALU.mult, [ft("uu"), ft("e1")], [ft("kkn")])
                            P.op("dve", lambda e: e.tensor_tensor_scan(out=B["fc"][:, :], data0=rmask[:, :], data1=B["sig"][:, :],
                                                                      initial=0.0, op0=ALU.mult, op1=ALU.add),
                                 [T_("rmask"), ft("sig")], [ft("fc")])
                            yield 0
                            tt("dve", f("uu"), f("fc"), f("sig"), ALU.subtract, [ft("fc"), ft("sig")], [ft("uu")])
                            totb = v3(f("fc"))[:, :, 63:64].to_broadcast([128, 4, 64])
                            tt("dve", v3(f("ww")), totb, v3(f("fc")), ALU.subtract, [ft("fc")], [ft("ww")])
                            if d == 0:
                                inc, exc, exo = "fc", "uu", "ww"
                            else:
                                tt("dve", f("inc"), f("ww"), f("sig"), ALU.add, [ft("ww"), ft("sig")], [ft("inc")])
                                inc, exc, exo = "inc", "ww", "uu"
                            act(f("e1"), f(inc), AF.Exp, [ft(inc)], [ft("e1")], scale=-CDEC)
                            act(f("e2"), f(exc), AF.Exp, [ft(exc)], [ft("e2")], scale=-CDEC)
                            act(f("e3"), f(inc), AF.Exp, [ft(inc)], [ft("e3")], scale=CDEC)
                            act(f("e4"), f(exo), AF.Exp, [ft(exo)], [ft("e4")], scale=-CDEC)
                            act(wc[:, :], v3(f("fc"))[:, :, 63], AF.Exp, [ft("fc")], [X("wc")], scale=-CDEC)
                            yield 0
                            ts("dve", f("fc"), f("aa"), -1.0, pc(l, "k_a", hp), ALU.add, ALU.mult, [ft("aa"), T_("pcol")], [ft("fc")])
                            stt(f("kdir"), f("fc"), 1.0, f("ks"), ALU.add, ALU.mult, [ft("fc"), ft("ks")], [ft("kdir")])
                            tt("dve", f("sig"), f("kkn"), f("aa"), ALU.mult, [ft("kkn"), ft("aa")], [ft("sig")])
                            stt(f("inc"), f("rs"), pc(l, "r_k", hp), f("kdir"), ALU.mult, ALU.mult,
                                [ft("rs"), ft("kdir"), T_("pcol")], [ft("inc")])
                            b = nextbank(5)
                            mm(bk(b)[:, 0:256], bones_f[:, :], f("inc"), True, True, [T_("bones_f"), ft("inc")], [bank_tok[b]])
                            if first:
                                tt("dve", bon0[:, tsl], bk(b)[:, 0:256], f("vs"), ALU.mult, [bank_tok[b], ft("vs")], [T_(f"bst{st}")])
                            else:
                                tt("dve", B["bon"][q][:, :], bk(b)[:, 0:256], f("vs"), ALU.mult, [bank_tok[b], ft("vs")], [X("bon")])
                            yield 0
                            for h in range(2):
                                hs = HB[h]
                                eng = "dve"

                                def dst(buf, h=h, hs=hs):
                                    return buf[hs, :, h * 64:(h + 1) * 64]

                                stt(AR[hs, :, 0, h * 64:(h + 1) * 64], v3(B["kkn"][hs, :]), -1.0, v3(B["e2"][hs, :]), ALU.mult, ALU.mult,
                                    [ft("kkn"), ft("e2")], [X("AR")])
                                tt(eng, AR[hs, :, 1, h * 64:(h + 1) * 64], v3(B["rs"][hs, :]), v3(B["e1"][hs, :]), ALU.mult,
                                   [ft("rs"), ft("e1")], [X("AR")])
                                tt(eng, dst(KT), v3(B["kdir"][hs, :]), v3(B["e3"][hs, :]), ALU.mult, [ft("kdir"), ft("e3")], [X("KT")])
                                tt(eng, dst(BT), v3(B["sig"][hs, :]), v3(B["e3"][hs, :]), ALU.mult, [ft("sig"), ft("e3")], [X("BT")])
                                tt(eng, dst(KH), v3(B["kdir"][hs, :]), v3(B["e4"][hs, :]), ALU.mult, [ft("kdir"), ft("e4")], [X("KH")])
                                tt(eng, dst(BH), v3(B["sig"][hs, :]), v3(B["e4"][hs, :]), ALU.mult, [ft("sig"), ft("e4")], [X("BH")])
                                cp(eng, dst(VT), v3(B["vs"][hs, :]), [ft("vs")], [X("VT")])
                            SY["prep"][d] = idx + 1
                            yield 0

                    def rwkv_gen(d, l=l, hp=hp, c0=c0):
                        B = TB[d]
                        Atok, KHtok, RB, Vtok, AakT, ArkT = B["Atok"], B["KHtok"], B["RB"], B["Vtok"], B["AakT"], B["ArkT"]
                        QP, Qb, GT, ObT, Hs, Sst, Sbf = B["QP"], B["Qb"], B["GT"], B["ObT"], B["Hs"], B["Sst"], B["Sbf"]

                        def XM(n):
                            return T_(f"{n}_{d}")

                        P.op("pool", lambda e: e.memset(Sst[:, :], 0.0), [], [XM("Sst")])
                        P.op("pool", lambda e: e.memset(Sbf[:, :], 0.0), [], [XM("Sbf")])
                        ms, mi, mso = (0, 1, 2) if d == 0 else (2, 3, 0)
                        st_order = list(range(NST)) if d == 0 else list(range(NST - 1, -1, -1))
                        for idx, st in enumerate(st_order):
                            q = idx % 2
                            while SY["prep"][d] <= idx:
                                yield 0
                            AR, KT, BT, KH, BH, VT, wc = B["AR"][q], B["KT"][q], B["BT"][q], B["KH"][q], B["BH"][q], B["VT"][q], B["wc"][q]
                            OPN = ("AR", "KT", "BT", "KH", "BH", "VT", "wc", "bon")

                            def X(n):
                                if n in OPN:
                                    return T_(f"{n}{q}_{d}")
                                return T_(f"{n}_{d}")

                            first = (st < NST // 2) if d == 0 else (st >= NST // 2)
                            t0 = st * 256
                            tsl = slice(t0, t0 + 256)

                            def mk(ix, n=4):
                                return cmask[:, ix, :].unsqueeze(1).to_broadcast([128, n, 128])

                            for src, srct, dstb, dstt in ((AR, "AR", Atok, "Atok"), (KH, "KH", KHtok, "KHtok"),
                                                          (BH, "BH", RB, "RBb"), (VT, "VT", Vtok, "Vtok")):
                                b = nextbank(5)
                                pbb = bk(b).bitcast(BF16)
                                for c in range(4):
                                    sap = src[:, c, 0, :] if src is AR else src[:, c, :]
                                    tr(pbb[:, c * 128:(c + 1) * 128], sap, [X(srct)], [bank_tok[b]])
                                dap = dstb[:, :, 1, :] if dstb is RB else dstb[:, :, :]
                                cp("act", dap, pbb[:, 0:512].rearrange("p (c k) -> p c k", k=128), [bank_tok[b]], [X(dstt)])
                            yield 0
                            b1 = nextbank(5); b2 = nextbank(5)
                            for c in range(4):
                                bb_ = b1 if c < 2 else b2
                                mm(bk(bb_)[:, (c % 2) * 256:(c % 2) * 256 + 256], KT[:, c, :],
                                   AR[:, c, :, :].rearrange("p a k -> p (a k)"), True, True, [X("KT"), X("AR")], [bank_tok[bb_]])
                            for bb_, cs_ in ((b1, slice(0, 2)), (b2, slice(2, 4))):
                                pv = bk(bb_)[:, :].rearrange("p (c a k) -> p c a k", a=2, k=128)
                                tt("dve", AakT[:, cs_, :], pv[:, :, 0, :], mk(ms, 2), ALU.mult, [bank_tok[bb_], T_("cmask")], [X("AakT")])
                                tt("dve", ArkT[:, cs_, :], pv[:, :, 1, :], mk(mi, 2), ALU.mult, [bank_tok[bb_], T_("cmask")], [X("ArkT")])
                            yield 0
                            b1 = nextbank(5); b2 = nextbank(5)
                            for c in range(4):
                                bb_ = b1 if c < 2 else b2
                                mm(bk(bb_)[:, (c % 2) * 256:(c % 2) * 256 + 256], BT[:, c, :],
                                   AR[:, c, :, :].rearrange("p a k -> p (a k)"), True, True, [X("BT"), X("AR")], [bank_tok[bb_]])
                            for bb_, cs_ in ((b1, slice(0, 2)), (b2, slice(2, 4))):
                                pv = bk(bb_)[:, :].rearrange("p (c a k) -> p c a k", a=2, k=128)
                                tt("dve", QP[0][:, cs_, 0, :], pv[:, :, 0, :], mk(ms, 2), ALU.mult, [bank_tok[bb_], T_("cmask")], [X("QP0")])
                                tt("dve", RB[:, cs_, 0, :], pv[:, :, 1, :], mk(mi, 2), ALU.mult, [bank_tok[bb_], T_("cmask")], [X("RBa")])
                            b = nextbank(5)
                            for c in range(4):
                                mm(bk(b)[:, c * 128:(c + 1) * 128], AR[:, c, 0, :], BT[:, c, :], True, True,
                                   [X("AR"), X("BT")], [bank_tok[b]])
                            tt("dve", Qb[0][:, :, :], pb3(b), mk(mso), ALU.mult, [bank_tok[b], T_("cmask")], [X("Qb0")])
                            cp("pool", QP[0][:, :, 1, :], ident_b[:, :].unsqueeze(1).to_broadcast([128, 4, 128]),
                               [T_("ident_b")], [X("QP0")])
                            yield 0
                            for lv in range(6):
                                ci, ni = lv % 2, (lv + 1) % 2
                                last = lv == 5
                                b1 = nextbank(5); b2 = nextbank(5)
                                for c in range(4):
                                    bb_ = b1 if c < 2 else b2
                                    if last:
                                        mm(bk(bb_)[:, (c % 2) * 256 + 128:(c % 2) * 256 + 256], Qb[ci][:, c, :],
                                           QP[ci][:, c, 1, :], True, True, [X(f"Qb{ci}"), X(f"QP{ci}")], [bank_tok[bb_]])
                                    else:
                                        mm(bk(bb_)[:, (c % 2) * 256:(c % 2) * 256 + 256], Qb[ci][:, c, :],
                                           QP[ci][:, c, :, :].rearrange("p a k -> p (a k)"), True, True,
                                           [X(f"Qb{ci}"), X(f"QP{ci}")], [bank_tok[bb_]])
                                if not last:
                                    b = nextbank(5)
                                    for c in range(4):
                                        mm(bk(b)[:, c * 128:(c + 1) * 128], QP[ci][:, c, 0, :], Qb[ci][:, c, :], True, True,
                                           [X(f"QP{ci}"), X(f"Qb{ci}")], [bank_tok[b]])
                                for bb_, cs_ in ((b1, slice(0, 2)), (b2, slice(2, 4))):
                                    pv = bk(bb_)[:, :].rearrange("p (c a k) -> p c a k", a=2, k=128)
                                    if not last:
                                        cp("act", QP[ni][:, cs_, 0, :], pv[:, :, 0, :], [bank_tok[bb_]], [X(f"QP{ni}")])
                                    tt("dve", QP[ni][:, cs_, 1, :], pv[:, :, 1, :], QP[ci][:, cs_, 1, :], ALU.add,
                                       [bank_tok[bb_], X(f"QP{ci}")], [X(f"QP{ni}")])
                                if not last:
                                    cp("act", Qb[ni][:, :, :], pb3(b), [bank_tok[b]], [X(f"Qb{ni}")])
                                yield 0
                            MT = QP[0]
                            MTt = X("QP0")
                            Abar = QP[1][:, :, 0, :]
                            P1 = QP[1][:, :, 1, :]
                            Ubar = Qb[0]
                            RbT = Qb[1]
                            b = nextbank(5)
                            for c in range(4):
                                mm(bk(b)[:, c * 128:(c + 1) * 128], MT[:, c, 1, :], Atok[:, c, :], True, True,
                                   [MTt, X("Atok")], [bank_tok[b]])
                            bq = nextbank(5)
                            for c in range(4):
                                mm(bk(bq)[:, c * 128:(c + 1) * 128], AakT[:, c, :], Vtok[:, c, :], True, True,
                                   [X("AakT"), X("Vtok")], [bank_tok[bq]])
                            cp("act", Abar, pb3(b), [bank_tok[b]], [X("QP1")])
                            cp("dve", P1, pb3(bq), [bank_tok[bq]], [X("QP1")])
                            yield 0
                            b = nextbank(5)
                            for c in range(4):
                                mm(bk(b)[:, c * 128:(c + 1) * 128], MT[:, c, 1, :], QP[1][:, c, 1, :], True, True,
                                   [MTt, X("QP1")], [bank_tok[b]])
                            b1 = nextbank(5); b2 = nextbank(5)
                            for c in range(4):
                                bb_ = b1 if c < 2 else b2
                                mm(bk(bb_)[:, (c % 2) * 256:(c % 2) * 256 + 256], QP[1][:, c, 0, :],
                                   RB[:, c, :, :].rearrange("p a k -> p (a k)"), True, True,
                                   [X("QP1"), X("RBa"), X("RBb")], [bank_tok[bb_]])
                            cp("act", Ubar[:, :, :], pb3(b), [bank_tok[b]], [X("Qb0")])
                            for bb_, cs_ in ((b1, slice(0, 2)), (b2, slice(2, 4))):
                                pv = bk(bb_)[:, :].rearrange("p (c a k) -> p c a k", a=2, k=128)
                                tt("dve", RbT[:, cs_, :], pv[:, :, 0, :], AR[:, cs_, 1, :], ALU.add, [bank_tok[bb_], X("AR")], [X("Qb1")])
                                for c in (cs_.start, cs_.start + 1):
                                    stt(GT[:, c, :], ident_f[:, :], wc[:, c:c + 1], pv[:, c % 2, 1, :], ALU.mult, ALU.add,
                                        [bank_tok[bb_], T_("ident_f"), X("wc")], [X("GT")])
                            yield 0
                            b = nextbank(5)
                            for c in range(4):
                                mm(bk(b)[:, c * 128:(c + 1) * 128], Vtok[:, c, :], ArkT[:, c, :], True, False,
                                   [X("Vtok"), X("ArkT")], [bank_tok[b]])
                                mm(bk(b)[:, c * 128:(c + 1) * 128], Ubar[:, c, :], RB[:, c, 0, :], False, True,
                                   [X("Qb0"), X("RBa")], [bank_tok[b]])
                            bq = nextbank(5)
                            for c in range(4):
                                mm(bk(bq)[:, c * 128:(c + 1) * 128], KHtok[:, c, :], Vtok[:, c, :], True, False,
                                   [X("KHtok"), X("Vtok")], [bank_tok[bq]])
                                mm(bk(bq)[:, c * 128:(c + 1) * 128], RB[:, c, 1, :], Ubar[:, c, :], False, True,
                                   [X("RBb"), X("Qb0")], [bank_tok[bq]])
                            for h in range(2):
                                cp("act", v3(ObT[HB[h], :]), pb3(b)[HB[h], :, h * 64:(h + 1) * 64], [bank_tok[b]], [X("ObT")])
                            cp("act", Hs[:, :, :], pb3(bq), [bank_tok[bq]], [X("Hs")])
                            yield 0
                            corder = range(4) if d == 0 else range(3, -1, -1)
                            bO = 6 + d
                            for c in corder:
                                mm(bk(bO)[:, c * 128:(c + 1) * 128], Sbf[:, :], RbT[:, c, :], True, True,
                                   [X("Sbf"), X("Qb1")], [bank_tok[bO]])
                                bs = nextbank(5)
                                sreg = bk(bs)[:, 0:128]
                                mm(sreg, GT[:, c, :], Sst[:, :], True, True, [X("GT"), X("Sst")], [bank_tok[bs]])
                                tt("dve", Sst[:, :], sreg, Hs[:, c, :], ALU.add, [bank_tok[bs], X("Hs")], [X("Sst")])
                                cp("act", Sbf[:, :], Sst[:, :], [X("Sst")], [X("Sbf")])
                                yield 0
                            if first:
                                for h in range(2):
                                    tt("dve", v3(o0T[HB[h], tsl]), pb3(bO)[HB[h], :, h * 64:(h + 1) * 64], v3(ObT[HB[h], :]), ALU.add,
                                       [bank_tok[bO], X("ObT")], [T_(f"ost{st}")])
                                SY["out"].add(st)
                            else:
                                while st not in SY["out"]:
                                    yield 0

                                def tmpf(nm):
                                    return B[nm][:, :, :].rearrange("p c k -> p (c k)").bitcast(F32)

                                oo, osq, mean, m2, var = tmpf("Atok"), tmpf("KHtok"), tmpf("Vtok"), tmpf("AakT"), tmpf("ArkT")
                                rstd = QP[1][:, :, 0, :].rearrange("p c k -> p (c k)") if False else None
                                q1 = QP[1][:, :, :, :].rearrange("p c a k -> p (c a k)").bitcast(F32)
                                rstd, xn = q1[:, 0:256], q1[:, 256:512]
                                oo3 = B["Atok"][:, :, :].rearrange("p c k -> p (c k)").bitcast(F32)
                                for h in range(2):
                                    tt("dve", v3(oo3[HB[h], :]), pb3(bO)[HB[h], :, h * 64:(h + 1) * 64], v3(ObT[HB[h], :]), ALU.add,
                                       [bank_tok[bO], X("ObT")], [X("Atok")])
                                tt("dve", oo, oo, o0T[:, tsl], ALU.add, [X("Atok"), T_(f"ost{st}")], [X("Atok")])
                                tt("dve", osq, oo, oo, ALU.mult, [X("Atok")], [X("KHtok")])
                                b = nextbank(5)
                                mm(bk(b)[:, 0:256], bones_f[:, :], oo, True, True, [T_("bones_f"), X("Atok")], [bank_tok[b]])
                                mm(bk(b)[:, 256:512], bones_f[:, :], osq, True, True, [T_("bones_f"), X("KHtok")], [bank_tok[b]])
                                act(mean, bk(b)[:, 0:256], AF.Copy, [bank_tok[b]], [X("Vtok")], scale=1.0 / 64)
                                tt("dve", m2, mean, mean, ALU.mult, [X("Vtok")], [X("AakT")])
                                stt(var, bk(b)[:, 256:512], 1.0 / 64, m2, ALU.mult, ALU.subtract,
                                    [bank_tok[b], X("AakT")], [X("ArkT")])
                                yield 0
                                ts("dve", var, var, 0.0, None, ALU.max, None, [X("ArkT")], [X("ArkT")])
                                act(rstd, var, AF.Ln, [X("ArkT")], [X("QP1")], bias=float(GN_EPS))
                                act(rstd, rstd, AF.Exp, [X("QP1")], [X("QP1")], scale=-0.5)
                                tt("dve", xn, oo, mean, ALU.subtract, [X("Atok"), X("Vtok")], [X("QP1")])
                                tt("dve", xn, xn, rstd, ALU.mult, [X("QP1")], [X("QP1")])
                                ts("dve", xn, xn, pc(l, "gn_w", hp), pc(l, "gn_b", hp), ALU.mult, ALU.add,
                                   [X("QP1"), T_("pcol")], [X("QP1")])
                                tt("dve", xn, xn, B["bon"][q][:, :], ALU.add, [X("QP1"), X("bon")], [X("QP1")])
                                tt("dve", xn, xn, bon0[:, tsl], ALU.add, [X("QP1"), T_(f"bst{st}")], [X("QP1")])
                                tt("dve", yaT[:, hp, tsl], xn, sga[:, tsl], ALU.mult, [X("QP1"), T_("sga")], [T_("yaT")])
                            SY["main"][d] = idx + 1
                            yield 1

                    gens = [prep_gen(0), prep_gen(1), rwkv_gen(0), rwkv_gen(1)]
                    alive = [True] * 4

                    def adv(gi):
                        try:
                            next(gens[gi])
                        except StopIteration:
                            alive[gi] = False

                    guard = 0
                    NFILL = 2
                    while any(alive):
                        for gi in range(4):
                            if alive[gi]:
                                adv(gi)
                        for _f in range(NFILL):
                            mm(bk(5)[:, 0:512], ident_b[:, :], hT[:, 0, 1:513], True, True, [T_("ident_b"), T_("hT")], [bank_tok[5]])
                        guard += 1
                        assert guard < 100000
                chk(8)
                P.barrier()
                for cb_ in range(8):
                    wma = load_w(w_l[:, OFF_MA + cb_ * 128:OFF_MA + (cb_ + 1) * 128], 8)
                    wmb = load_w(w_l[:, OFF_MB + cb_ * 128:OFF_MB + (cb_ + 1) * 128], 8)
                    wpa = load_w(wpa_d[l][:, cb_ * 128:(cb_ + 1) * 128], 4)
                    wpb = load_w(wpb_d[l][:, cb_ * 128:(cb_ + 1) * 128], 4)
                    for t4 in range(T // 512):
                        k = 0
                        tsl = slice(t4 * 512, (t4 + 1) * 512)
                        hsl = slice(1 + t4 * 512, 1 + (t4 + 1) * 512)
                        b = nextbank()
                        zmm(b, 512, wma, 8, lambda kc, hsl=hsl: hT[:, kc, hsl], T_("hT"))
                        act(msg[k][:, 0, :], bk(b)[:, :], AF.Sigmoid, [bank_tok[b]], [T_(f"msg{k}0")])
                        b = nextbank()
                        zmm(b, 512, wmb, 8, lambda kc, hsl=hsl: hT[:, kc, hsl], T_("hT"))
                        act(msg[k][:, 1, :], bk(b)[:, :], AF.Sigmoid, [bank_tok[b]], [T_(f"msg{k}1")])
                        b = nextbank()
                        zmm(b, 512, wpa, 4, lambda kc, tsl=tsl: yaT[:, kc, tsl], T_("yaT"))
                        tt("dve", msg[k][:, 2, :], bk(b)[:, :], msg[k][:, 0, :], ALU.mult, [bank_tok[b], T_(f"msg{k}0")],
                           [T_(f"msg{k}2")])
                        b = nextbank()
                        zmm(b, 512, wpb, 4, lambda kc, tsl=tsl: ybT[:, kc, tsl], T_("ybT"))
                        tt("dve", msg[k][:, 3, :], bk(b)[:, :], msg[k][:, 1, :], ALU.mult, [bank_tok[b], T_(f"msg{k}1")],
                           [T_(f"msg{k}3")])
                        tt("pool", mergedT[:, cb_, tsl], msg[k][:, 2, :], msg[k][:, 3, :], ALU.add,
                           [T_(f"msg{k}2"), T_(f"msg{k}3")], [T_("mergedT")])
                chk(9)
                for kc in range(8):
                    dma("pool", "d_wo", woutT[:, kc, :], wout_d[l][kc * 128:(kc + 1) * 128, :], [], [T_("woutT")])
                src_d = x_d if l == 0 else x1_d
                load_g(1 if l == 0 else 2, 1)
                for i in range(NT):
                    k = i % 2
                    rd = [T_(f"x1d{s}_{i}")] if l == 1 else []
                    dma("sp", f"d_x{k}", xst[k][:, :], src_d[s, i * 128:(i + 1) * 128, :], rd, [T_(f"xst{k}")])
                    for half in range(2):
                        b = nextbank()
                        for kc in range(8):
                            mm(bk(b)[:, :], mergedT[:, kc, i * 128:(i + 1) * 128], woutT[:, kc, half * 512:(half + 1) * 512],
                               kc == 0, kc == 7, [T_("mergedT"), T_("woutT")], [bank_tok[b]])
                        tt("dve", xnew[k][:, half * 512:(half + 1) * 512], bk(b)[:, :], xst[k][:, half * 512:(half + 1) * 512],
                           ALU.add, [bank_tok[b], T_(f"xst{k}")], [T_(f"xnew{k}")])
                    if l == 0:
                        dma("sp", f"d_o{k}", x1_d[s, i * 128:(i + 1) * 128, :], xnew[k][:, :], [T_(f"xnew{k}")], [T_(f"x1d{s}_{i}")])
                        norm_tile(xnew[k][:, :], T_(f"xnew{k}"), i, 1, RMS_EPS)
                    else:
                        norm_tile(xnew[k][:, :], T_(f"xnew{k}"), i, 1, RMS_EPS, out_f=yst[k][:, :], out_tok=T_(f"yst{k}"))
                        dma("sp", f"d_y{k}", y_d[s, i * 128:(i + 1) * 128, :], yst[k][:, :], [T_(f"yst{k}")], [T_(f"yd{s}_{i}")])

    try:
        chk(0)
        body()
    except _Stop:
        pass
    P.final_wait("sp")

    from contextlib import ExitStack
    with ExitStack() as es:
        sem = {}
        for k in list(P.cnt.keys()) + list(P.dcnt.keys()):
            sem[k] = es.enter_context(nc.semaphore(k))
        block = es.enter_context(nc.Block())

        def emit(name, e):
            for waits, fn, inc_key, inc in P.ops[name]:
                for k, v in waits.items():
                    e.wait_ge(sem[k], v)
                if fn is not None:
                    fn(e).then_inc(sem[inc_key], inc)

        @block.tensor
        def _(e):
            emit("pe", e)

        @block.scalar
        def _(e):
            emit("act", e)

        @block.vector
        def _(e):
            emit("dve", e)

        @block.gpsimd
        def _(e):
            emit("pool", e)

        @block.sync
        def _(e):
            emit("sp", e)
    return nc


def _consts():
    c = np.zeros((128, 1280), np.float32)
    p = np.arange(128)
    h = p // 64
    j = p % 64
    same = (h[:, None] == h[None, :])
    jj, ss = j[:, None], j[None, :]
    c[:, 0:128] = same & (jj < ss)
    c[:, 128:256] = same & (jj <= ss)
    c[:, 256:384] = same & (jj > ss)
    c[:, 384:512] = same & (jj >= ss)
    c[:, 512:640] = np.eye(128)
    c[:, 640:768] = same
    rm = np.ones(256, np.float32)
    rm[::64] = 0.0
    c[:, 768:1024] = rm[None, :]
    c[:, 1024:1152] = 1.0
    return c


E_ENTRIES = [(d, d + 1) for d in range(14)] + [(None, 3), (4, 5), (6, 7), (8, 9), (10, None)]


def _etables(rpb):
    Lr = rpb.shape[0]
    kc = np.arange(64)[:, None]
    qc = np.arange(64)[None, :]
    cs = np.clip(qc - 8, 0, 48)
    win = (kc >= cs) & (kc < cs + 16)
    dc = np.clip(kc - qc + 15, 0, 30)
    out = np.zeros((Lr, 4, 128, NE, 128), np.float32)
    mask = np.zeros((128, NE, 128), np.float32)
    for ei, pair in enumerate(E_ENTRIES):
        for half in range(2):
            dr = pair[half]
            if dr is None:
                continue
            for h in range(2):
                mask[half * 64:(half + 1) * 64, ei, h * 64:(h + 1) * 64] = win
                for hp in range(4):
                    out[:, hp, half * 64:(half + 1) * 64, ei, h * 64:(h + 1) * 64] = rpb[:, 2 * hp + h, dr][:, dc]
    return out.reshape(Lr, 4, 128, NE * 128), mask.reshape(128, NE * 128)


def _pack(inp):
    f = lambda a: np.ascontiguousarray(np.asarray(a, dtype=np.float32))
    pcol = np.zeros((128, L * PC_PER_L), np.float32)
    for l in range(L):
        b = l * PC_PER_L
        pcol[:, b:b + 14] = f(inp["shift_mu"])[l].reshape(14, 128).T
        pcol[:, b + 14:b + 22] = f(inp["w0"])[l].reshape(8, 128).T
        pcol[:, b + 22:b + 30] = f(inp["a0"])[l].reshape(8, 128).T
        pcol[:, b + 30:b + 34] = f(inp["k_k"])[l].reshape(4, 128).T
        pcol[:, b + 34:b + 38] = f(inp["k_a"])[l].reshape(4, 128).T
        pcol[:, b + 38:b + 42] = f(inp["r_k"])[l].reshape(4, 128).T
        pcol[:, b + 42:b + 46] = f(inp["gn_w"])[l].reshape(4, 128).T
        pcol[:, b + 46:b + 50] = f(inp["gn_b"])[l].reshape(4, 128).T
    rpbT, emask = _etables(f(inp["rpb"]))
    shared = {
        "w_in": f(inp["w_in"]), "w_pa": f(inp["w_pa"]), "w_pb": f(inp["w_pb"]), "w_out": f(inp["w_out"]),
        "gains": np.concatenate([f(inp["norm_g"]), f(inp["final_g"])[None, :]], 0),
        "pcol": pcol,
        "w2t": f(inp["w2"]).reshape(L, 128, 512), "a2t": f(inp["a2"]).reshape(L, 128, 512),
        "consts": _consts(), "emask": emask, "rpbT": rpbT,
    }
    return shared


_NC_CACHE = {}


def run(x_all, inp, T, NSEQ, ncores=NCORES):
    key = (T, NSEQ)
    if key not in _NC_CACHE:
        _NC_CACHE[key] = build_program(T, NSEQ)
    nc = _NC_CACHE[key]
    shared = _pack(inp)
    in_maps = []
    for c in range(ncores):
        m = dict(shared)
        m["x"] = np.ascontiguousarray(x_all[c * NSEQ:(c + 1) * NSEQ])
        in_maps.append(m)
    res = run_bass_kernel_spmd(nc, in_maps, core_ids=list(range(ncores)))
    return np.concatenate([r["y"] for r in res.results], 0), res


def kernel(x_prompt, x_sample, norm_g, w_in, shift_mu, w0, w2, a0, a2, k_k, k_a, r_k,
           gn_w, gn_b, rpb, w_pa, w_pb, w_out, final_g):
    inp = dict(norm_g=norm_g, w_in=w_in, shift_mu=shift_mu, w0=w0, w2=w2, a0=a0, a2=a2, k_k=k_k, k_a=k_a,
               r_k=r_k, gn_w=gn_w, gn_b=gn_b, rpb=rpb, w_pa=w_pa, w_pb=w_pb, w_out=w_out, final_g=final_g)
    xp = np.asarray(x_prompt, np.float32)
    xs = np.asarray(x_sample, np.float32)
    x_all = np.concatenate([xp, xs], 0)
    y, _ = run(x_all, inp, 2048, 3)
    return y[:xp.shape[0]].astype(np.float32), y[xp.shape[0]:].astype(np.float32)
```

```python
import numpy as np
import concourse.bass as bass
import concourse.mybir as mybir
from concourse.bass_utils import run_bass_kernel_spmd

F32 = mybir.dt.float32
BF16 = mybir.dt.bfloat16
AF = mybir.ActivationFunctionType
ALU = mybir.AluOpType
AX = mybir.AxisListType

D = 1024
DIN = 6400
L = 2
NCORES = 8
OFF_GA = 1792
OFF_QB = OFF_GA + 512
OFF_KB = OFF_QB + 512
OFF_VB = OFF_KB + 512
OFF_GB = OFF_VB + 512
OFF_MA = OFF_GB + 512
OFF_MB = OFF_MA + 1024
RMS_EPS = 1e-6
GN_EPS = 6.4e-4
CDEC = float(np.exp(-0.5))
NE = 19
PC_PER_L = 50


class Tok:
    __slots__ = ("w", "r", "name")

    def __init__(self, name=""):
        self.w = None
        self.r = {}
        self.name = name


class Prog:
    ENGS = ("pe", "act", "dve", "pool", "sp")

    def __init__(self):
        self.ops = {e: [] for e in self.ENGS}
        self.cnt = {e: 0 for e in ("pe", "act", "dve", "pool")}
        self.seen = {e: {} for e in self.ENGS}
        self.dcnt = {}
        self.pending = {e: {} for e in self.ENGS}

    def _need(self, eng, waits, key, val, raw, is_dma=False):
        if key == eng and not is_dma and eng == "pe":
            return
        if self.seen[eng].get(key, 0) >= val:
            return
        if waits.get(key, 0) < val:
            waits[key] = val

    def _deps(self, eng, reads, writes, is_dma=False):
        waits = dict(self.pending[eng])
        self.pending[eng] = {}
        for t in reads:
            if t.w is not None:
                self._need(eng, waits, t.w[0], t.w[1], True, is_dma)
        for t in writes:
            if t.w is not None:
                self._need(eng, waits, t.w[0], t.w[1], False, is_dma)
            for k, v in t.r.items():
                self._need(eng, waits, k, v, False, is_dma)
        for k, v in waits.items():
            if self.seen[eng].get(k, 0) < v:
                self.seen[eng][k] = v
        return waits

    def op(self, eng, fn, reads=(), writes=()):
        waits = self._deps(eng, reads, writes)
        self.cnt[eng] += 1
        c = self.cnt[eng]
        for t in reads:
            if t.r.get(eng, 0) < c:
                t.r[eng] = c
        for t in writes:
            t.w = (eng, c)
            t.r = {}
        self.ops[eng].append((waits, fn, eng, 1))

    def dma(self, queue, dsem, fn, reads=(), writes=()):
        waits = self._deps(queue, reads, writes, True)
        self.dcnt[dsem] = self.dcnt.get(dsem, 0) + 16
        c = self.dcnt[dsem]
        for t in reads:
            if t.r.get(dsem, 0) < c:
                t.r[dsem] = c
        for t in writes:
            t.w = (dsem, c)
            t.r = {}
        self.ops[queue].append((waits, fn, dsem, 16))

    def barrier(self):
        snap = dict(self.cnt)
        snap.update(self.dcnt)
        for e in self.ENGS:
            for k, v in snap.items():
                if k == e:
                    continue
                if self.seen[e].get(k, 0) < v and self.pending[e].get(k, 0) < v:
                    self.pending[e][k] = v

    def final_wait(self, eng="sp"):
        waits = {}
        for k, v in self.dcnt.items():
            waits[k] = v
        for k, v in self.cnt.items():
            waits[k] = v
        self.ops[eng].append((waits, None, None, 0))


def build_program(T, NSEQ):
    assert T % 512 == 0
    NT = T // 128
    NST = T // 256
    ROWS = T // 64
    nc = bass.Bass("TRN2", target_bir_lowering=False)
    P = Prog()

    x_d = nc.dram_tensor("x", [NSEQ, T, D], F32, kind="ExternalInput").ap()
    y_d = nc.dram_tensor("y", [NSEQ, T, D], F32, kind="ExternalOutput").ap()
    x1_d = nc.dram_tensor("x1s", [NSEQ, T, D], F32, kind="ExternalOutput").ap()
    win_d = nc.dram_tensor("w_in", [L, D, DIN], F32, kind="ExternalInput").ap()
    wpa_d = nc.dram_tensor("w_pa", [L, 512, D], F32, kind="ExternalInput").ap()
    wpb_d = nc.dram_tensor("w_pb", [L, 512, D], F32, kind="ExternalInput").ap()
    wout_d = nc.dram_tensor("w_out", [L, D, D], F32, kind="ExternalInput").ap()
    g_d = nc.dram_tensor("gains", [3, D], F32, kind="ExternalInput").ap()
    pcol_d = nc.dram_tensor("pcol", [128, L * PC_PER_L], F32, kind="ExternalInput").ap()
    w2_d = nc.dram_tensor("w2t", [L, 128, 512], F32, kind="ExternalInput").ap()
    a2_d = nc.dram_tensor("a2t", [L, 128, 512], F32, kind="ExternalInput").ap()
    cst_d = nc.dram_tensor("consts", [128, 1280], F32, kind="ExternalInput").ap()
    emask_d = nc.dram_tensor("emask", [128, NE * 128], F32, kind="ExternalInput").ap()
    rpb_d = nc.dram_tensor("rpbT", [L, 4, 128, NE * 128], F32, kind="ExternalInput").ap()

    class Bump:
        def __init__(self, start):
            self.off = start

        def alloc(self, name, shape, dt):
            nbytes = int(np.prod(shape[1:])) * (4 if dt == F32 else 2)
            nbytes = (nbytes + 31) // 32 * 32
            t = nc.alloc_sbuf_tensor_at(name, list(shape), dt, offset=self.off)
            self.off += nbytes
            return t

    pers = Bump(17408)
    hT = pers.alloc("hT", [128, 8, T + 2], BF16)
    yaT = pers.alloc("yaT", [128, 4, T], BF16)
    ybT = pers.alloc("ybT", [128, 4, T], BF16)
    NW = 5
    wslot = [pers.alloc(f"ws{i}", [128, 8, 128], BF16) for i in range(NW)]
    cmask = pers.alloc("cmask", [128, 4, 128], BF16)
    ident_b = pers.alloc("ident_b", [128, 128], BF16)
    ident_f = pers.alloc("ident_f", [128, 128], F32)
    bones_f = pers.alloc("bones_f", [128, 128], F32)
    ones_b = pers.alloc("ones_b", [128, 128], BF16)
    rmask = pers.alloc("rmask", [128, 256], F32)
    pcol = pers.alloc("pcol", [128, L * PC_PER_L], F32)
    pc2 = pers.alloc("pc2", [128, L * 28], F32)
    _w2 = [pers.alloc(f"w2t{d_}", [128, 512], BF16) for d_ in range(2)]
    _a2 = [pers.alloc(f"a2t{d_}", [128, 512], BF16) for d_ in range(2)]
    w2t = [_w2 for l in range(L)]
    a2t = [_a2 for l in range(L)]
    PH0 = pers.off

    phN = Bump(PH0)
    mergedT = phN.alloc("mergedT", [128, 8, T], BF16)
    woutT = phN.alloc("woutT", [128, 8, D], BF16)
    xst = [phN.alloc(f"xst{i}", [128, D], F32) for i in range(2)]
    xnew = [phN.alloc(f"xnew{i}", [128, D], F32) for i in range(2)]
    yst = [phN.alloc(f"yst{i}", [128, D], F32) for i in range(2)]
    junk = phN.alloc("junk", [128, D], BF16)
    htok = [phN.alloc(f"htok{i}", [128, D], BF16) for i in range(2)]
    stat = phN.alloc("stat", [128, 16], F32)
    msg = [phN.alloc("msg0", [128, 4, 512], F32)] * 2
    grep = [phN.alloc(f"grep{i}", [128, D], F32) for i in range(2)]
    cst_f = phN.alloc("cst_f", [128, 1280], F32)
    endN = phN.off

    phH = Bump(PH0)
    loraW = phH.alloc("loraW", [128, T], BF16)
    loraA = phH.alloc("loraA", [128, T], BF16)
    sga = phH.alloc("sga", [128, T], BF16)
    zs = [phH.alloc(f"zs{i}", [128, 258], F32) for i in range(2)]
    HP0 = phH.off
    phA = Bump(HP0)
    qT = phA.alloc("qT", [128, T // 64, 128], BF16)
    kT = phA.alloc("kT", [128, T], BF16)
    sgb = phA.alloc("sgb", [128, T], BF16)
    vbt = phA.alloc("vbt", [128, NT, 128], BF16)
    rpbb = phA.alloc("rpbb", [128, NE, 128], BF16)
    etf = phA.alloc("etf", [128, NE, 128], F32)
    etab = phA.alloc("etab", [128, NE, 128], BF16)
    emask = phA.alloc("emask", [128, NE, 128], BF16)
    exb = [phA.alloc(f"exb{i}", [128, 5, 128], BF16) for i in range(3)]
    ptb = [phA.alloc(f"ptb{i}", [128, 5, 128], BF16) for i in range(3)]
    rden = [phA.alloc(f"rden{i}", [128, 64], F32) for i in range(3)]
    g2 = [phA.alloc(f"g2{i}", [128, 64], F32) for i in range(3)]
    endA = phA.off
    phR = Bump(HP0)
    o0T = phR.alloc("o0T", [128, T], F32)
    bon0 = phR.alloc("bon0", [128, T], BF16)
    FN = ("rs", "ks", "vs", "kkn", "sig", "aa", "fc", "uu", "ww", "inc", "e1", "e2", "e3", "e4", "kdir")
    TB = []
    for d_ in range(2):
        B = {}
        for nm in FN:
            B[nm] = phR.alloc(f"f{d_}_" + nm, [128, 256], F32)
        B["wc"] = [phR.alloc(f"wc{d_}{q}", [128, 4], F32) for q in range(2)]
        B["bon"] = [phR.alloc(f"bon{d_}{q}", [128, 256], BF16) for q in range(2)]
        B["AR"] = [phR.alloc(f"AR{d_}{q}", [128, 4, 2, 128], BF16) for q in range(2)]
        for nm in ("KT", "BT", "KH", "BH", "VT"):
            B[nm] = [phR.alloc(f"{nm}{d_}{q}", [128, 4, 128], BF16) for q in range(2)]
        for nm in ("Atok", "KHtok", "Vtok", "AakT", "ArkT"):
            B[nm] = phR.alloc(f"{nm}{d_}", [128, 4, 128], BF16)
        B["RB"] = phR.alloc(f"RB{d_}", [128, 4, 2, 128], BF16)
        B["QP"] = [phR.alloc(f"QP{d_}{i}", [128, 4, 2, 128], BF16) for i in range(2)]
        B["Qb"] = [phR.alloc(f"Qb{d_}{i}", [128, 4, 128], BF16) for i in range(2)]
        B["GT"] = phR.alloc(f"GT{d_}", [128, 4, 128], F32)
        B["ObT"] = phR.alloc(f"ObT{d_}", [128, 256], F32)
        B["Hs"] = phR.alloc(f"Hs{d_}", [128, 4, 128], F32)
        B["Sst"] = phR.alloc(f"Sst{d_}", [128, 128], F32)
        B["Sbf"] = phR.alloc(f"Sbf{d_}", [128, 128], BF16)
        TB.append(B)
    endR = phR.off
    top = max(endN, endA, endR)
    assert top <= 229376, (PH0, endN, endA, endR)

    banks = [nc.alloc_psum_tensor(f"pb{i}", [128, 512], F32) for i in range(8)]
    bank_tok = [Tok(f"bank{i}") for i in range(8)]
    rr = [0]

    def bk(i):
        return banks[i][:, :]

    def nextbank(n=6):
        i = rr[0] % n
        rr[0] += 1
        return i

    tk = {}

    def T_(name):
        if name not in tk:
            tk[name] = Tok(name)
        return tk[name]

    def mm(out, lhsT, rhs, start, stop, reads, writes):
        P.op("pe", lambda e, o=out, l=lhsT, r=rhs, s=start, t=stop: e.matmul(o, lhsT=l, rhs=r, start=s, stop=t),
             reads, writes)

    def tr(out, in_, reads, writes):
        P.op("pe", lambda e, o=out, i=in_: e.transpose(out=o, in_=i, identity=ident_b[:, :]),
             reads + [T_("ident_b")], writes)

    def act(out, in_, func, reads, writes, bias=None, scale=None):
        kw = {}
        if bias is not None:
            kw["bias"] = bias
        if scale is not None:
            kw["scale"] = scale
        P.op("act", lambda e, o=out, i=in_, f=func, kw=kw: e.activation(out=o, in_=i, func=f, **kw), reads, writes)

    def tt(eng, out, in0, in1, op, reads, writes):
        P.op(eng, lambda e, o=out, a=in0, b=in1, p=op: e.tensor_tensor(out=o, in0=a, in1=b, op=p), reads, writes)

    def ts(eng, out, in0, s1, s2, op0, op1, reads, writes):
        if op1 is None:
            P.op(eng, lambda e, o=out, a=in0, s=s1, p=op0: e.tensor_scalar(out=o, in0=a, scalar1=s, scalar2=None, op0=p),
                 reads, writes)
        else:
            P.op(eng, lambda e, o=out, a=in0, s=s1, q=s2, p=op0, r=op1:
                 e.tensor_scalar(out=o, in0=a, scalar1=s, scalar2=q, op0=p, op1=r), reads, writes)

    def stt(out, in0, scalar, in1, op0, op1, reads, writes):
        P.op("dve", lambda e, o=out, a=in0, s=scalar, b=in1, p=op0, q=op1:
             e.scalar_tensor_tensor(out=o, in0=a, scalar=s, in1=b, op0=p, op1=q), reads, writes)

    def cp(eng, out, in_, reads, writes):
        if eng == "act":
            P.op("act", lambda e, o=out, i=in_: e.activation(out=o, in_=i, func=AF.Copy), reads, writes)
        else:
            P.op(eng, lambda e, o=out, i=in_: e.tensor_copy(out=o, in_=i), reads, writes)

    def recip(out, in_, reads, writes):
        P.op("dve", lambda e, o=out, i=in_: e.reciprocal(out=o, in_=i), reads, writes)

    def dma(queue, dsem, out, in_, reads, writes):
        P.dma(queue, dsem, lambda e, o=out, i=in_: e.dma_start(out=o, in_=i), reads, writes)

    t_c = T_("consts")
    dma("sp", "d_c0", cst_f[:, :], cst_d[:, :], [], [T_("cst_f")])
    dma("sp", "d_c1", pcol[:, :], pcol_d[:, :], [], [T_("pcol")])
    for d_ in range(2):
        P.op("pool", lambda e, t=_w2[d_]: e.memset(t[:, :], 0.0), [], [T_("w2t")])
        P.op("pool", lambda e, t=_a2[d_]: e.memset(t[:, :], 0.0), [], [T_("w2t")])

    def load_lora2(l):
        for d_ in range(2):
            hs_ = slice(d_ * 64, (d_ + 1) * 64)
            dma("pool", "d_c3", _w2[d_][hs_, :], w2_d[l][hs_, :], [], [T_("w2t")])
            dma("pool", "d_c3", _a2[d_][hs_, :], a2_d[l][hs_, :], [], [T_("w2t")])

    cp("dve", cmask[:, :, :], cst_f[:, 0:512].rearrange("p (a b) -> p a b", b=128), [T_("cst_f")], [T_("cmask")])
    cp("dve", ident_b[:, :], cst_f[:, 512:640], [T_("cst_f")], [T_("ident_b")])
    cp("dve", ident_f[:, :], cst_f[:, 512:640], [T_("cst_f")], [T_("ident_f")])
    cp("dve", bones_f[:, :], cst_f[:, 640:768], [T_("cst_f")], [T_("bones_f")])
    cp("dve", rmask[:, :], cst_f[:, 768:1024], [T_("cst_f")], [T_("rmask")])
    cp("dve", ones_b[:, :], cst_f[:, 1024:1152], [T_("cst_f")], [T_("ones_b")])
    for l in range(L):
        ts("dve", pc2[:, l * 28:l * 28 + 14], pcol[:, l * PC_PER_L:l * PC_PER_L + 14], -1.0, 1.0, ALU.mult, ALU.add,
           [T_("pcol")], [T_("pc2")])
        ts("dve", pc2[:, l * 28 + 14:l * 28 + 28], pcol[:, l * PC_PER_L:l * PC_PER_L + 14], 0.5, None, ALU.mult, None,
           [T_("pcol")], [T_("pc2")])
    P.op("pool", lambda e: e.memset(hT[:, :, 0:1], 0.0), [], [T_("hTpad")])
    P.op("pool", lambda e: e.memset(hT[:, :, T + 1:T + 2], 0.0), [], [T_("hTpad")])

    def pc(l, name, idx=0):
        base = l * PC_PER_L
        offs = {"mu": 0, "w0": 14, "a0": 22, "k_k": 30, "k_a": 34, "r_k": 38, "gn_w": 42, "gn_b": 46}
        c = base + offs[name] + idx
        return pcol[:, c:c + 1]

    wtok = [Tok(f"ws{i}") for i in range(NW)]
    wrr = [0]

    def load_w(src_ap, kcs):
        i = wrr[0] % NW
        wrr[0] += 1
        dma("pool", f"d_w{i}", wslot[i][:, 0:kcs, :], src_ap.rearrange("(k p) c -> p k c", p=128), [], [wtok[i]])
        return i

    def load_g(which, slot):
        dma("sp", f"d_g{slot}", grep[slot][:, :], g_d[which, :].partition_broadcast(128), [], [T_(f"grep{slot}")])

    def norm_tile(src, src_tok, i, gslot, eps, out_h=None, out_f=None, out_tok=None):
        k = i % 2
        if i == 2:
            chk(109)
        chk(100)
        act(junk[:, :], src, AF.Square, [src_tok], [T_("junk")])
        chk(101)
        P.op("dve", lambda e, i=i: e.tensor_reduce(out=stat[:, i % 16:i % 16 + 1], in_=junk[:, :], axis=AX.X, op=ALU.add),
             [T_("junk")], [T_(f"stat{i % 16}")])
        chk(102)
        act(stat[:, i % 16:i % 16 + 1], stat[:, i % 16:i % 16 + 1], AF.Sqrt, [T_(f"stat{i % 16}")], [T_(f"stat{i % 16}")],
            bias=float(eps), scale=1.0 / D)
        chk(103)
        recip(stat[:, i % 16:i % 16 + 1], stat[:, i % 16:i % 16 + 1], [T_(f"stat{i % 16}")], [T_(f"stat{i % 16}")])
        chk(104)
        if out_f is not None:
            stt(out_f, src, stat[:, i % 16:i % 16 + 1], grep[gslot][:, :], ALU.mult, ALU.mult,
                [src_tok, T_(f"stat{i % 16}"), T_(f"grep{gslot}")], [out_tok])
            return
        stt(htok[k][:, :], src, stat[:, i % 16:i % 16 + 1], grep[gslot][:, :], ALU.mult, ALU.mult,
            [src_tok, T_(f"stat{i % 16}"), T_(f"grep{gslot}")], [T_(f"htok{k}")])
        chk(105)
        b = nextbank()
        pb = bk(b).bitcast(BF16)
        for kc in range(8):
            tr(pb[:, kc * 128:(kc + 1) * 128], htok[k][:, kc * 128:(kc + 1) * 128], [T_(f"htok{k}")], [bank_tok[b]])
        chk(106)
        cp("act", hT[:, :, 1 + i * 128:1 + (i + 1) * 128], pb[:, 0:1024].rearrange("p (k t) -> p k t", t=128),
           [bank_tok[b]], [T_("hT")])
        chk(107)
        if i == 1:
            chk(108)
        if i == 2:
            chk(110)
        if i == 3:
            chk(111)

    def zmm(b, ncols, wi, kcs, rhs_fn, rhs_tok, col0=0, extra=()):
        for kc in range(kcs):
            mm(bk(b)[:, col0:col0 + ncols], wslot[wi][:, kc, :], rhs_fn(kc), kc == 0, kc == kcs - 1,
               [wtok[wi], rhs_tok] + list(extra), [bank_tok[b]])

    def shifted_block(l, wi, mublk, t0, out_ap, out_tok, k):
        b = nextbank()
        zmm(b, 258, wi, 8, lambda kc: hT[:, kc, t0:t0 + 258], T_("hT"), extra=[T_("hTpad")])
        cp("act", zs[k][:, :], bk(b)[:, 0:258], [bank_tok[b]], [T_(f"zs{k}")])
        tt("dve", TB[k]["uu"][:, :], zs[k][:, 0:256], zs[k][:, 2:258], ALU.add, [T_(f"zs{k}")], [T_(f"f{k}_uu")])
        ts("dve", TB[k]["uu"][:, :], TB[k]["uu"][:, :], pc2[:, l * 28 + 14 + mublk:l * 28 + 15 + mublk], None, ALU.mult, None,
           [T_(f"f{k}_uu"), T_("pc2")], [T_(f"f{k}_uu")])
        stt(out_ap, zs[k][:, 1:257], pc2[:, l * 28 + mublk:l * 28 + mublk + 1], TB[k]["uu"][:, :], ALU.mult, ALU.add,
            [T_(f"zs{k}"), T_(f"f{k}_uu"), T_("pc2")], [out_tok])

    HB = [slice(0, 64), slice(64, 128)]
    import os as _os
    KSTOP = int(_os.environ.get("KSTOP", "99"))

    class _Stop(Exception):
        pass

    def chk(stage):
        if stage == KSTOP:
            raise _Stop()

    def body():

        for s in range(NSEQ):
            for l in range(L):
                if l == 0:
                    P.barrier()
                    load_g(0, 0)
                    for i in range(NT):
                        k = i % 2
                        dma("sp", f"d_x{k}", xst[k][:, :], x_d[s, i * 128:(i + 1) * 128, :], [], [T_(f"xst{k}")])
                        norm_tile(xst[k][:, :], T_(f"xst{k}"), i, 0, RMS_EPS)
                P.barrier()
                w_l = win_d[l]
                load_lora2(l)
                for j in range(2):
                    wi = load_w(w_l[:, 1536 + j * 128:1536 + (j + 1) * 128], 8)
                    for st in range(NST):
                        k = st % 2
                        shifted_block(l, wi, 12 + j, st * 256, TB[k]["e1"][:, :], T_(f"f{k}_e1"), k)
                        if j == 0:
                            act(loraW[:, st * 256:(st + 1) * 256], TB[k]["e1"][:, :], AF.Tanh, [T_(f"f{k}_e1")], [T_("loraW")])
                        else:
                            cp("act", loraA[:, st * 256:(st + 1) * 256], TB[k]["e1"][:, :], [T_(f"f{k}_e1")], [T_("loraA")])
                chk(2)
                for hp in range(4):
                    c0 = hp * 128
                    P.barrier()
                    dma("pool", "d_c2", emask[:, :, :], emask_d.rearrange("p (e c) -> p e c", c=128), [], [T_("emask")])
                    dma("pool", "d_rpb", rpbb[:, :, :], rpb_d[l, hp].rearrange("p (e c) -> p e c", c=128), [], [T_("rpbb")])
                    act(etf[:, :, :], rpbb[:, :, :], AF.Exp, [T_("rpbb")], [T_("etf")])
                    tt("dve", etab[:, :, :], etf[:, :, :], emask[:, :, :], ALU.mult, [T_("etf"), T_("emask")], [T_("etab")])
                    P.op("pool", lambda e: e.memset(qT[:, :, :], 0.0), [], [T_("qT")])
                    wq = load_w(w_l[:, OFF_QB + c0:OFF_QB + c0 + 128], 8)
                    wk = load_w(w_l[:, OFF_KB + c0:OFF_KB + c0 + 128], 8)
                    wv = load_w(w_l[:, OFF_VB + c0:OFF_VB + c0 + 128], 8)
                    wgb = load_w(w_l[:, OFF_GB + c0:OFF_GB + c0 + 128], 8)
                    wga = load_w(w_l[:, OFF_GA + c0:OFF_GA + c0 + 128], 8)
                    for t4 in range(T // 512):
                        tsl = slice(t4 * 512, (t4 + 1) * 512)
                        hsl = slice(1 + t4 * 512, 1 + (t4 + 1) * 512)
                        b = nextbank()
                        zmm(b, 512, wq, 8, lambda kc, hsl=hsl: hT[:, kc, hsl], T_("hT"))
                        for h in range(2):
                            act(qT[HB[h], t4 * 8:(t4 + 1) * 8, h * 64:(h + 1) * 64],
                                bk(b)[HB[h], :].rearrange("p (r q) -> p r q", q=64), AF.Copy,
                                [bank_tok[b]], [T_("qT")], scale=0.125)
                        b = nextbank()
                        zmm(b, 512, wk, 8, lambda kc, hsl=hsl: hT[:, kc, hsl], T_("hT"))
                        cp("dve", kT[:, tsl], bk(b)[:, :], [bank_tok[b]], [T_("kT")])
                        b = nextbank()
                        zmm(b, 512, wgb, 8, lambda kc, hsl=hsl: hT[:, kc, hsl], T_("hT"))
                        act(sgb[:, tsl], bk(b)[:, :], AF.Silu, [bank_tok[b]], [T_("sgb")])
                        b = nextbank()
                        zmm(b, 512, wga, 8, lambda kc, hsl=hsl: hT[:, kc, hsl], T_("hT"))
                        act(sga[:, tsl], bk(b)[:, :], AF.Silu, [bank_tok[b]], [T_("sga")])
                        b = nextbank()
                        for q4 in range(4):
                            i = t4 * 4 + q4
                            for kc in range(8):
                                mm(bk(b)[:, q4 * 128:(q4 + 1) * 128], hT[:, kc, 1 + i * 128:1 + (i + 1) * 128],
                                   wslot[wv][:, kc, :], kc == 0, kc == 7, [T_("hT"), wtok[wv]], [bank_tok[b]])
                        cp("dve", vbt[:, t4 * 4:(t4 + 1) * 4, :], bk(b)[:, :].rearrange("p (a c) -> p a c", c=128),
                           [bank_tok[b]], [T_("vbt")])
                    chk(3)
                    NB_ROW = 3

                    def row_front(r):
                        k = r % NB_ROW
                        rs_ = min(max(r - 4, 0), ROWS - 8)
                        if rs_ % 2 == 0:
                            nt_, tile0, e0, estep = 4, rs_ // 2, rs_ - r + 7, 2
                        else:
                            nt_, tile0, e0, estep = 5, (rs_ - 1) // 2, 14, 1
                        bA = nextbank()
                        bB = nextbank() if nt_ == 5 else None
                        for i in range(nt_):
                            bb_, col = (bA, i * 128) if i < 4 else (bB, 0)
                            mm(bk(bb_)[:, col:col + 128], kT[:, (tile0 + i) * 128:(tile0 + i + 1) * 128], qT[:, r, :],
                               True, True, [T_("kT"), T_("qT")], [bank_tok[bb_]])
                        act(exb[k][:, 0:4, :], bk(bA)[:, :].rearrange("p (a c) -> p a c", c=128), AF.Exp,
                            [bank_tok[bA]], [T_(f"exb{k}")])
                        if nt_ == 5:
                            act(exb[k][:, 4, :], bk(bB)[:, 0:128], AF.Exp, [bank_tok[bB]], [T_(f"exb{k}")])
                        tt("dve", ptb[k][:, 0:nt_, :], exb[k][:, 0:nt_, :], etab[:, e0:e0 + estep * (nt_ - 1) + 1:estep, :],
                           ALU.mult, [T_(f"exb{k}"), T_("etab")], [T_(f"ptb{k}")])
                        return nt_, tile0

                    def row_back(r, nt_, tile0):
                        k = r % NB_ROW
                        bO = nextbank()
                        for i in range(nt_):
                            mm(bk(bO)[:, 0:128], vbt[:, tile0 + i, :], ptb[k][:, i, :], i == 0, i == nt_ - 1,
                               [T_("vbt"), T_(f"ptb{k}")], [bank_tok[bO]])
                        for i in range(nt_):
                            mm(bk(bO)[:, 128:256], ones_b[:, :], ptb[k][:, i, :], i == 0, i == nt_ - 1,
                               [T_("ones_b"), T_(f"ptb{k}")], [bank_tok[bO]])
                        for h in range(2):
                            recip(rden[k][HB[h], :], bk(bO)[HB[h], 128 + h * 64:128 + (h + 1) * 64],
                                  [bank_tok[bO]], [T_(f"rden{k}")])
                        tt("dve", g2[k][:, :], rden[k][:, :], sgb[:, r * 64:(r + 1) * 64], ALU.mult,
                           [T_(f"rden{k}"), T_("sgb")], [T_(f"g2{k}")])
                        for h in range(2):
                            tt("dve", ybT[HB[h], hp, r * 64:(r + 1) * 64], bk(bO)[HB[h], h * 64:(h + 1) * 64],
                               g2[k][HB[h], :], ALU.mult, [bank_tok[bO], T_(f"g2{k}")], [T_("ybT")])

                    LOOK = 2
                    fr = {}
                    for r in range(min(LOOK, ROWS)):
                        fr[r] = row_front(r)
                    for r in range(ROWS):
                        if r + LOOK < ROWS:
                            fr[r + LOOK] = row_front(r + LOOK)
                        row_back(r, *fr.pop(r))
                    chk(4)
                    P.barrier()
                    wr = load_w(w_l[:, c0:c0 + 128], 8)
                    wkk = load_w(w_l[:, 512 + c0:512 + c0 + 128], 8)
                    wvv = load_w(w_l[:, 1024 + c0:1024 + c0 + 128], 8)

                    def v3(ap):
                        return ap.rearrange("p (c k) -> p c k", k=64)

                    def pb3(b, lo=0, hi=512):
                        return bk(b)[:, lo:hi].rearrange("p (c k) -> p c k", k=128)

                    SY = {"prep": [0, 0], "main": [0, 0], "out": set()}

                    def prep_gen(d, l=l, hp=hp, c0=c0, wr=wr, wkk=wkk, wvv=wvv):
                        B = TB[d]

                        def f(n):
                            return B[n][:, :]

                        def ft(n):
                            return T_(f"f{d}_{n}")

                        for q in range(2):
                            for nm in ("AR", "KT", "BT", "KH", "BH", "VT"):
                                buf = B[nm][q]
                                P.op("pool", lambda e, buf=buf: e.memset(buf[tuple(slice(None) for _ in buf.shape)], 0.0), [],
                                     [T_(f"{nm}{q}_{d}")])
                        st_order = list(range(NST)) if d == 0 else list(range(NST - 1, -1, -1))
                        for idx, st in enumerate(st_order):
                            q = idx % 2
                            while SY["main"][d] < idx - 1:
                                yield 0
                            AR, KT, BT, KH, BH, VT, wc = B["AR"][q], B["KT"][q], B["BT"][q], B["KH"][q], B["BH"][q], B["VT"][q], B["wc"][q]

                            def X(n):
                                return T_(f"{n}{q}_{d}")

                            first = (st < NST // 2) if d == 0 else (st >= NST // 2)
                            t0 = st * 256
                            tsl = slice(t0, t0 + 256)
                            shifted_block(l, wr, hp, t0, f("rs"), ft("rs"), d)
                            yield 0
                            shifted_block(l, wkk, 4 + hp, t0, f("ks"), ft("ks"), d)
                            yield 0
                            shifted_block(l, wvv, 8 + hp, t0, f("vs"), ft("vs"), d)
                            yield 0
                            ts("dve", f("uu"), f("ks"), pc(l, "k_k", hp), None, ALU.mult, None, [ft("ks"), T_("pcol")], [ft("uu")])
                            tt("dve", f("ww"), f("uu"), f("uu"), ALU.mult, [ft("uu")], [ft("ww")])
                            b = nextbank(6)
                            mm(bk(b)[:, 0:256], bones_f[:, :], f("ww"), True, True, [T_("bones_f"), ft("ww")], [bank_tok[b]])
                            b2_ = nextbank(6)
                            mm(bk(b2_)[:, 0:256], w2t[l][d][:, c0:c0 + 128], loraW[:, tsl], True, True,
                               [T_("w2t"), T_("loraW")], [bank_tok[b2_]])
                            mm(bk(b2_)[:, 256:512], a2t[l][d][:, c0:c0 + 128], loraA[:, tsl], True, True,
                               [T_("w2t"), T_("loraA")], [bank_tok[b2_]])
                            act(f("sig"), bk(b2_)[:, 0:256], AF.Sigmoid, [bank_tok[b2_], T_("pcol")], [ft("sig")],
                                bias=pc(l, "w0", d * 4 + hp))
                            act(f("aa"), bk(b2_)[:, 256:512], AF.Sigmoid, [bank_tok[b2_], T_("pcol")], [ft("aa")],
                                bias=pc(l, "a0", d * 4 + hp))
                            act(f("e1"), bk(b)[:, 0:256], AF.Ln, [bank_tok[b]], [ft("e1")], bias=1e-24)
                            act(f("e1"), f("e1"), AF.Exp, [ft("e1")], [ft("e1")], scale=-0.5)
                            yield 0
                            tt("dve", f("kkn"), f("uu"), f("e1"), ALU.mult, [ft("uu"), ft("e1")], [ft("kkn")])
                            P.op("dve", lambda e: e.tensor_tensor_scan(out=B["fc"][:, :], data0=rmask[:, :], data1=B["sig"][:, :],
                                                                      initial=0.0, op0=ALU.mult, op1=ALU.add),
                                 [T_("rmask"), ft("sig")], [ft("fc")])
                            yield 0
                            yield 0
                            tt("dve", f("uu"), f("fc"), f("sig"), ALU.subtract, [ft("fc"), ft("sig")], [ft("uu")])
                            totb = v3(f("fc"))[:, :, 63:64].to_broadcast([128, 4, 64])
                            tt("dve", v3(f("ww")), totb, v3(f("fc")), ALU.subtract, [ft("fc")], [ft("ww")])
                            if d == 0:
                                inc, exc, exo = "fc", "uu", "ww"
                            else:
                                tt("dve", f("inc"), f("ww"), f("sig"), ALU.add, [ft("ww"), ft("sig")], [ft("inc")])
                                inc, exc, exo = "inc", "ww", "uu"
                            act(f("e1"), f(inc), AF.Exp, [ft(inc)], [ft("e1")], scale=-CDEC)
                            act(f("e2"), f(exc), AF.Exp, [ft(exc)], [ft("e2")], scale=-CDEC)
                            act(f("e3"), f(inc), AF.Exp, [ft(inc)], [ft("e3")], scale=CDEC)
                            act(f("e4"), f(exo), AF.Exp, [ft(exo)], [ft("e4")], scale=-CDEC)
                            act(wc[:, :], v3(f("fc"))[:, :, 63], AF.Exp, [ft("fc")], [X("wc")], scale=-CDEC)
                            yield 0
                            ts("dve", f("fc"), f("aa"), -1.0, pc(l, "k_a", hp), ALU.add, ALU.mult, [ft("aa"), T_("pcol")], [ft("fc")])
                            stt(f("kdir"), f("fc"), 1.0, f("ks"), ALU.add, ALU.mult, [ft("fc"), ft("ks")], [ft("kdir")])
                            tt("dve", f("sig"), f("kkn"), f("aa"), ALU.mult, [ft("kkn"), ft("aa")], [ft("sig")])
                            yield 0
                            stt(f("inc"), f("rs"), pc(l, "r_k", hp), f("kdir"), ALU.mult, ALU.mult,
                                [ft("rs"), ft("kdir"), T_("pcol")], [ft("inc")])
                            b = nextbank(6)
                            mm(bk(b)[:, 0:256], bones_f[:, :], f("inc"), True, True, [T_("bones_f"), ft("inc")], [bank_tok[b]])
                            if first:
                                tt("dve", bon0[:, tsl], bk(b)[:, 0:256], f("vs"), ALU.mult, [bank_tok[b], ft("vs")], [T_(f"bst{st}")])
                            else:
                                tt("dve", B["bon"][q][:, :], bk(b)[:, 0:256], f("vs"), ALU.mult, [bank_tok[b], ft("vs")], [X("bon")])
                            yield 0
                            for h in range(2):
                                hs = HB[h]
                                eng = "dve"

                                def dst(buf, h=h, hs=hs):
                                    return buf[hs, :, h * 64:(h + 1) * 64]

                                stt(AR[hs, :, 0, h * 64:(h + 1) * 64], v3(B["kkn"][hs, :]), -1.0, v3(B["e2"][hs, :]), ALU.mult, ALU.mult,
                                    [ft("kkn"), ft("e2")], [X("AR")])
                                tt(eng, AR[hs, :, 1, h * 64:(h + 1) * 64], v3(B["rs"][hs, :]), v3(B["e1"][hs, :]), ALU.mult,
                                   [ft("rs"), ft("e1")], [X("AR")])
                                yield 0
                                tt(eng, dst(KT), v3(B["kdir"][hs, :]), v3(B["e3"][hs, :]), ALU.mult, [ft("kdir"), ft("e3")], [X("KT")])
                                tt(eng, dst(BT), v3(B["sig"][hs, :]), v3(B["e3"][hs, :]), ALU.mult, [ft("sig"), ft("e3")], [X("BT")])
                                yield 0
                                tt(eng, dst(KH), v3(B["kdir"][hs, :]), v3(B["e4"][hs, :]), ALU.mult, [ft("kdir"), ft("e4")], [X("KH")])
                                tt(eng, dst(BH), v3(B["sig"][hs, :]), v3(B["e4"][hs, :]), ALU.mult, [ft("sig"), ft("e4")], [X("BH")])
                                yield 0
                                cp(eng, dst(VT), v3(B["vs"][hs, :]), [ft("vs")], [X("VT")])
                            SY["prep"][d] = idx + 1
                            yield 0

                    def rwkv_gen(d, l=l, hp=hp, c0=c0):
                        B = TB[d]
                        Atok, KHtok, RB, Vtok, AakT, ArkT = B["Atok"], B["KHtok"], B["RB"], B["Vtok"], B["AakT"], B["ArkT"]
                        QP, Qb, GT, ObT, Hs, Sst, Sbf = B["QP"], B["Qb"], B["GT"], B["ObT"], B["Hs"], B["Sst"], B["Sbf"]

                        def XM(n):
                            return T_(f"{n}_{d}")

                        P.op("pool", lambda e: e.memset(Sst[:, :], 0.0), [], [XM("Sst")])
                        P.op("pool", lambda e: e.memset(Sbf[:, :], 0.0), [], [XM("Sbf")])
                        ms, mi, mso = (0, 1, 2) if d == 0 else (2, 3, 0)
                        st_order = list(range(NST)) if d == 0 else list(range(NST - 1, -1, -1))
                        for idx, st in enumerate(st_order):
                            q = idx % 2
                            while SY["prep"][d] <= idx:
                                yield 0
                            AR, KT, BT, KH, BH, VT, wc = B["AR"][q], B["KT"][q], B["BT"][q], B["KH"][q], B["BH"][q], B["VT"][q], B["wc"][q]
                            OPN = ("AR", "KT", "BT", "KH", "BH", "VT", "wc", "bon")

                            def X(n):
                                if n in OPN:
                                    return T_(f"{n}{q}_{d}")
                                return T_(f"{n}_{d}")

                            first = (st < NST // 2) if d == 0 else (st >= NST // 2)
                            t0 = st * 256
                            tsl = slice(t0, t0 + 256)

                            def mk(ix, n=4):
                                return cmask[:, ix, :].unsqueeze(1).to_broadcast([128, n, 128])

                            for src, srct, dstb, dstt in ((AR, "AR", Atok, "Atok"), (KH, "KH", KHtok, "KHtok"),
                                                          (BH, "BH", RB, "RBb"), (VT, "VT", Vtok, "Vtok")):
                                b = nextbank(6)
                                pbb = bk(b).bitcast(BF16)
                                for c in range(4):
                                    sap = src[:, c, 0, :] if src is AR else src[:, c, :]
                                    tr(pbb[:, c * 128:(c + 1) * 128], sap, [X(srct)], [bank_tok[b]])
                                dap = dstb[:, :, 1, :] if dstb is RB else dstb[:, :, :]
                                cp("act", dap, pbb[:, 0:512].rearrange("p (c k) -> p c k", k=128), [bank_tok[b]], [X(dstt)])
                            yield 0
                            b1 = nextbank(6); b2 = nextbank(6)
                            for c in range(4):
                                bb_ = b1 if c < 2 else b2
                                mm(bk(bb_)[:, (c % 2) * 256:(c % 2) * 256 + 256], KT[:, c, :],
                                   AR[:, c, :, :].rearrange("p a k -> p (a k)"), True, True, [X("KT"), X("AR")], [bank_tok[bb_]])
                            for bb_, cs_ in ((b1, slice(0, 2)), (b2, slice(2, 4))):
                                pv = bk(bb_)[:, :].rearrange("p (c a k) -> p c a k", a=2, k=128)
                                tt("dve", AakT[:, cs_, :], pv[:, :, 0, :], mk(ms, 2), ALU.mult, [bank_tok[bb_], T_("cmask")], [X("AakT")])
                                tt("dve", ArkT[:, cs_, :], pv[:, :, 1, :], mk(mi, 2), ALU.mult, [bank_tok[bb_], T_("cmask")], [X("ArkT")])
                            yield 0
                            b1 = nextbank(6); b2 = nextbank(6)
                            for c in range(4):
                                bb_ = b1 if c < 2 else b2
                                mm(bk(bb_)[:, (c % 2) * 256:(c % 2) * 256 + 256], BT[:, c, :],
                                   AR[:, c, :, :].rearrange("p a k -> p (a k)"), True, True, [X("BT"), X("AR")], [bank_tok[bb_]])
                            for bb_, cs_ in ((b1, slice(0, 2)), (b2, slice(2, 4))):
                                pv = bk(bb_)[:, :].rearrange("p (c a k) -> p c a k", a=2, k=128)
                                tt("dve", QP[0][:, cs_, 0, :], pv[:, :, 0, :], mk(ms, 2), ALU.mult, [bank_tok[bb_], T_("cmask")], [X("QP0")])
                                tt("dve", RB[:, cs_, 0, :], pv[:, :, 1, :], mk(mi, 2), ALU.mult, [bank_tok[bb_], T_("cmask")], [X("RBa")])
                            b = nextbank(6)
                            for c in range(4):
                                mm(bk(b)[:, c * 128:(c + 1) * 128], AR[:, c, 0, :], BT[:, c, :], True, True,
                                   [X("AR"), X("BT")], [bank_tok[b]])
                            tt("dve", Qb[0][:, :, :], pb3(b), mk(mso), ALU.mult, [bank_tok[b], T_("cmask")], [X("Qb0")])
                            cp("pool", QP[0][:, :, 1, :], ident_b[:, :].unsqueeze(1).to_broadcast([128, 4, 128]),
                               [T_("ident_b")], [X("QP0")])
                            yield 0
                            for lv in range(6):
                                ci, ni = lv % 2, (lv + 1) % 2
                                last = lv == 5
                                b1 = nextbank(6); b2 = nextbank(6)
                                for c in range(4):
                                    bb_ = b1 if c < 2 else b2
                                    if last:
                                        mm(bk(bb_)[:, (c % 2) * 256 + 128:(c % 2) * 256 + 256], Qb[ci][:, c, :],
                                           QP[ci][:, c, 1, :], True, True, [X(f"Qb{ci}"), X(f"QP{ci}")], [bank_tok[bb_]])
                                    else:
                                        mm(bk(bb_)[:, (c % 2) * 256:(c % 2) * 256 + 256], Qb[ci][:, c, :],
                                           QP[ci][:, c, :, :].rearrange("p a k -> p (a k)"), True, True,
                                           [X(f"Qb{ci}"), X(f"QP{ci}")], [bank_tok[bb_]])
                                if not last:
                                    b = nextbank(6)
                                    for c in range(4):
                                        mm(bk(b)[:, c * 128:(c + 1) * 128], QP[ci][:, c, 0, :], Qb[ci][:, c, :], True, True,
                                           [X(f"QP{ci}"), X(f"Qb{ci}")], [bank_tok[b]])
                                for bb_, cs_ in ((b1, slice(0, 2)), (b2, slice(2, 4))):
                                    pv = bk(bb_)[:, :].rearrange("p (c a k) -> p c a k", a=2, k=128)
                                    if not last:
                                        cp("act", QP[ni][:, cs_, 0, :], pv[:, :, 0, :], [bank_tok[bb_]], [X(f"QP{ni}")])
                                    tt("dve", QP[ni][:, cs_, 1, :], pv[:, :, 1, :], QP[ci][:, cs_, 1, :], ALU.add,
                                       [bank_tok[bb_], X(f"QP{ci}")], [X(f"QP{ni}")])
                                if not last:
                                    cp("act", Qb[ni][:, :, :], pb3(b), [bank_tok[b]], [X(f"Qb{ni}")])
                                yield 0
                            MT = QP[0]
                            MTt = X("QP0")
                            Abar = QP[1][:, :, 0, :]
                            P1 = QP[1][:, :, 1, :]
                            Ubar = Qb[0]
                            RbT = Qb[1]
                            b = nextbank(6)
                            for c in range(4):
                                mm(bk(b)[:, c * 128:(c + 1) * 128], MT[:, c, 1, :], Atok[:, c, :], True, True,
                                   [MTt, X("Atok")], [bank_tok[b]])
                            bq = nextbank(6)
                            for c in range(4):
                                mm(bk(bq)[:, c * 128:(c + 1) * 128], AakT[:, c, :], Vtok[:, c, :], True, True,
                                   [X("AakT"), X("Vtok")], [bank_tok[bq]])
                            cp("act", Abar, pb3(b), [bank_tok[b]], [X("QP1")])
                            cp("dve", P1, pb3(bq), [bank_tok[bq]], [X("QP1")])
                            yield 0
                            b = nextbank(6)
                            for c in range(4):
                                mm(bk(b)[:, c * 128:(c + 1) * 128], MT[:, c, 1, :], QP[1][:, c, 1, :], True, True,
                                   [MTt, X("QP1")], [bank_tok[b]])
                            b1 = nextbank(6); b2 = nextbank(6)
                            for c in range(4):
                                bb_ = b1 if c < 2 else b2
                                mm(bk(bb_)[:, (c % 2) * 256:(c % 2) * 256 + 256], QP[1][:, c, 0, :],
                                   RB[:, c, :, :].rearrange("p a k -> p (a k)"), True, True,
                                   [X("QP1"), X("RBa"), X("RBb")], [bank_tok[bb_]])
                            cp("act", Ubar[:, :, :], pb3(b), [bank_tok[b]], [X("Qb0")])
                            for bb_, cs_ in ((b1, slice(0, 2)), (b2, slice(2, 4))):
                                pv = bk(bb_)[:, :].rearrange("p (c a k) -> p c a k", a=2, k=128)
                                tt("dve", RbT[:, cs_, :], pv[:, :, 0, :], AR[:, cs_, 1, :], ALU.add, [bank_tok[bb_], X("AR")], [X("Qb1")])
                                for c in (cs_.start, cs_.start + 1):
                                    stt(GT[:, c, :], ident_f[:, :], wc[:, c:c + 1], pv[:, c % 2, 1, :], ALU.mult, ALU.add,
                                        [bank_tok[bb_], T_("ident_f"), X("wc")], [X("GT")])
                            yield 0
                            b = nextbank(6)
                            for c in range(4):
                                mm(bk(b)[:, c * 128:(c + 1) * 128], Vtok[:, c, :], ArkT[:, c, :], True, False,
                                   [X("Vtok"), X("ArkT")], [bank_tok[b]])
                                mm(bk(b)[:, c * 128:(c + 1) * 128], Ubar[:, c, :], RB[:, c, 0, :], False, True,
                                   [X("Qb0"), X("RBa")], [bank_tok[b]])
                            bq = nextbank(6)
                            for c in range(4):
                                mm(bk(bq)[:, c * 128:(c + 1) * 128], KHtok[:, c, :], Vtok[:, c, :], True, False,
                                   [X("KHtok"), X("Vtok")], [bank_tok[bq]])
                                mm(bk(bq)[:, c * 128:(c + 1) * 128], RB[:, c, 1, :], Ubar[:, c, :], False, True,
                                   [X("RBb"), X("Qb0")], [bank_tok[bq]])
                            for h in range(2):
                                cp("act", v3(ObT[HB[h], :]), pb3(b)[HB[h], :, h * 64:(h + 1) * 64], [bank_tok[b]], [X("ObT")])
                            cp("act", Hs[:, :, :], pb3(bq), [bank_tok[bq]], [X("Hs")])
                            yield 0
                            corder = range(4) if d == 0 else range(3, -1, -1)
                            bO = 6 + d
                            for c in corder:
                                mm(bk(bO)[:, c * 128:(c + 1) * 128], Sbf[:, :], RbT[:, c, :], True, True,
                                   [X("Sbf"), X("Qb1")], [bank_tok[bO]])
                                bs = nextbank(6)
                                sreg = bk(bs)[:, 0:128]
                                mm(sreg, GT[:, c, :], Sst[:, :], True, True, [X("GT"), X("Sst")], [bank_tok[bs]])
                                tt("dve", Sst[:, :], sreg, Hs[:, c, :], ALU.add, [bank_tok[bs], X("Hs")], [X("Sst")])
                                cp("act", Sbf[:, :], Sst[:, :], [X("Sst")], [X("Sbf")])
                                yield 0
                            if first:
                                for h in range(2):
                                    tt("dve", v3(o0T[HB[h], tsl]), pb3(bO)[HB[h], :, h * 64:(h + 1) * 64], v3(ObT[HB[h], :]), ALU.add,
                                       [bank_tok[bO], X("ObT")], [T_(f"ost{st}")])
                                SY["out"].add(st)
                            else:
                                while st not in SY["out"]:
                                    yield 0

                                def tmpf(nm):
                                    return B[nm][:, :, :].rearrange("p c k -> p (c k)").bitcast(F32)

                                oo, osq, mean, m2, var = tmpf("Atok"), tmpf("KHtok"), tmpf("Vtok"), tmpf("AakT"), tmpf("ArkT")
                                rstd = QP[1][:, :, 0, :].rearrange("p c k -> p (c k)") if False else None
                                q1 = QP[1][:, :, :, :].rearrange("p c a k -> p (c a k)").bitcast(F32)
                                rstd, xn = q1[:, 0:256], q1[:, 256:512]
                                oo3 = B["Atok"][:, :, :].rearrange("p c k -> p (c k)").bitcast(F32)
                                for h in range(2):
                                    tt("dve", v3(oo3[HB[h], :]), pb3(bO)[HB[h], :, h * 64:(h + 1) * 64], v3(ObT[HB[h], :]), ALU.add,
                                       [bank_tok[bO], X("ObT")], [X("Atok")])
                                tt("dve", oo, oo, o0T[:, tsl], ALU.add, [X("Atok"), T_(f"ost{st}")], [X("Atok")])
                                tt("dve", osq, oo, oo, ALU.mult, [X("Atok")], [X("KHtok")])
                                b = nextbank(6)
                                mm(bk(b)[:, 0:256], bones_f[:, :], oo, True, True, [T_("bones_f"), X("Atok")], [bank_tok[b]])
                                mm(bk(b)[:, 256:512], bones_f[:, :], osq, True, True, [T_("bones_f"), X("KHtok")], [bank_tok[b]])
                                act(mean, bk(b)[:, 0:256], AF.Copy, [bank_tok[b]], [X("Vtok")], scale=1.0 / 64)
                                tt("dve", m2, mean, mean, ALU.mult, [X("Vtok")], [X("AakT")])
                                stt(var, bk(b)[:, 256:512], 1.0 / 64, m2, ALU.mult, ALU.subtract,
                                    [bank_tok[b], X("AakT")], [X("ArkT")])
                                yield 0
                                ts("dve", var, var, 0.0, None, ALU.max, None, [X("ArkT")], [X("ArkT")])
                                act(rstd, var, AF.Ln, [X("ArkT")], [X("QP1")], bias=float(GN_EPS))
                                act(rstd, rstd, AF.Exp, [X("QP1")], [X("QP1")], scale=-0.5)
                                tt("dve", xn, oo, mean, ALU.subtract, [X("Atok"), X("Vtok")], [X("QP1")])
                                tt("dve", xn, xn, rstd, ALU.mult, [X("QP1")], [X("QP1")])
                                ts("dve", xn, xn, pc(l, "gn_w", hp), pc(l, "gn_b", hp), ALU.mult, ALU.add,
                                   [X("QP1"), T_("pcol")], [X("QP1")])
                                tt("dve", xn, xn, B["bon"][q][:, :], ALU.add, [X("QP1"), X("bon")], [X("QP1")])
                                tt("dve", xn, xn, bon0[:, tsl], ALU.add, [X("QP1"), T_(f"bst{st}")], [X("QP1")])
                                tt("dve", yaT[:, hp, tsl], xn, sga[:, tsl], ALU.mult, [X("QP1"), T_("sga")], [T_("yaT")])
                            SY["main"][d] = idx + 1
                            yield 1

                    gens = [prep_gen(0), prep_gen(1), rwkv_gen(0), rwkv_gen(1)]
                    alive = [True] * 4

                    def adv(gi):
                        try:
                            next(gens[gi])
                        except StopIteration:
                            alive[gi] = False

                    guard = 0
                    for _ in range(12):
                        for gi in (0, 2):
                            if alive[gi]:
                                adv(gi)
                    while any(alive):
                        for gi in range(4):
                            if alive[gi]:
                                adv(gi)
                        guard += 1
                        assert guard < 100000
                chk(8)
                P.barrier()
                for cb_ in range(8):
                    wma = load_w(w_l[:, OFF_MA + cb_ * 128:OFF_MA + (cb_ + 1) * 128], 8)
                    wmb = load_w(w_l[:, OFF_MB + cb_ * 128:OFF_MB + (cb_ + 1) * 128], 8)
                    wpa = load_w(wpa_d[l][:, cb_ * 128:(cb_ + 1) * 128], 4)
                    wpb = load_w(wpb_d[l][:, cb_ * 128:(cb_ + 1) * 128], 4)
                    for t4 in range(T // 512):
                        k = 0
                        tsl = slice(t4 * 512, (t4 + 1) * 512)
                        hsl = slice(1 + t4 * 512, 1 + (t4 + 1) * 512)
                        b = nextbank()
                        zmm(b, 512, wma, 8, lambda kc, hsl=hsl: hT[:, kc, hsl], T_("hT"))
                        act(msg[k][:, 0, :], bk(b)[:, :], AF.Sigmoid, [bank_tok[b]], [T_(f"msg{k}0")])
                        b = nextbank()
                        zmm(b, 512, wmb, 8, lambda kc, hsl=hsl: hT[:, kc, hsl], T_("hT"))
                        act(msg[k][:, 1, :], bk(b)[:, :], AF.Sigmoid, [bank_tok[b]], [T_(f"msg{k}1")])
                        b = nextbank()
                        zmm(b, 512, wpa, 4, lambda kc, tsl=tsl: yaT[:, kc, tsl], T_("yaT"))
                        tt("dve", msg[k][:, 2, :], bk(b)[:, :], msg[k][:, 0, :], ALU.mult, [bank_tok[b], T_(f"msg{k}0")],
                           [T_(f"msg{k}2")])
                        b = nextbank()
                        zmm(b, 512, wpb, 4, lambda kc, tsl=tsl: ybT[:, kc, tsl], T_("ybT"))
                        tt("dve", msg[k][:, 3, :], bk(b)[:, :], msg[k][:, 1, :], ALU.mult, [bank_tok[b], T_(f"msg{k}1")],
                           [T_(f"msg{k}3")])
                        tt("pool", mergedT[:, cb_, tsl], msg[k][:, 2, :], msg[k][:, 3, :], ALU.add,
                           [T_(f"msg{k}2"), T_(f"msg{k}3")], [T_("mergedT")])
                chk(9)
                for kc in range(8):
                    dma("pool", "d_wo", woutT[:, kc, :], wout_d[l][kc * 128:(kc + 1) * 128, :], [], [T_("woutT")])
                src_d = x_d if l == 0 else x1_d
                load_g(1 if l == 0 else 2, 1)
                for i in range(NT):
                    k = i % 2
                    rd = [T_(f"x1d{s}_{i}")] if l == 1 else []
                    dma("sp", f"d_x{k}", xst[k][:, :], src_d[s, i * 128:(i + 1) * 128, :], rd, [T_(f"xst{k}")])
                    for half in range(2):
                        b = nextbank()
                        for kc in range(8):
                            mm(bk(b)[:, :], mergedT[:, kc, i * 128:(i + 1) * 128], woutT[:, kc, half * 512:(half + 1) * 512],
                               kc == 0, kc == 7, [T_("mergedT"), T_("woutT")], [bank_tok[b]])
                        tt("dve", xnew[k][:, half * 512:(half + 1) * 512], bk(b)[:, :], xst[k][:, half * 512:(half + 1) * 512],
                           ALU.add, [bank_tok[b], T_(f"xst{k}")], [T_(f"xnew{k}")])
                    if l == 0:
                        dma("sp", f"d_o{k}", x1_d[s, i * 128:(i + 1) * 128, :], xnew[k][:, :], [T_(f"xnew{k}")], [T_(f"x1d{s}_{i}")])
                        norm_tile(xnew[k][:, :], T_(f"xnew{k}"), i, 1, RMS_EPS)
                    else:
                        norm_tile(xnew[k][:, :], T_(f"xnew{k}"), i, 1, RMS_EPS, out_f=yst[k][:, :], out_tok=T_(f"yst{k}"))
                        dma("sp", f"d_y{k}", y_d[s, i * 128:(i + 1) * 128, :], yst[k][:, :], [T_(f"yst{k}")], [T_(f"yd{s}_{i}")])

    try:
        chk(0)
        body()
    except _Stop:
        pass
    P.final_wait("sp")

    from contextlib import ExitStack
    with ExitStack() as es:
        sem = {}
        for k in list(P.cnt.keys()) + list(P.dcnt.keys()):
            sem[k] = es.enter_context(nc.semaphore(k))
        block = es.enter_context(nc.Block())

        def emit(name, e):
            for waits, fn, inc_key, inc in P.ops[name]:
                for k, v in waits.items():
                    e.wait_ge(sem[k], v)
                if fn is not None:
                    fn(e).then_inc(sem[inc_key], inc)

        @block.tensor
        def _(e):
            emit("pe", e)

        @block.scalar
        def _(e):
            emit("act", e)

        @block.vector
        def _(e):
            emit("dve", e)

        @block.gpsimd
        def _(e):
            emit("pool", e)

        @block.sync
        def _(e):
            emit("sp", e)
    return nc


def _consts():
    c = np.zeros((128, 1280), np.float32)
    p = np.arange(128)
    h = p // 64
    j = p % 64
    same = (h[:, None] == h[None, :])
    jj, ss = j[:, None], j[None, :]
    c[:, 0:128] = same & (jj < ss)
    c[:, 128:256] = same & (jj <= ss)
    c[:, 256:384] = same & (jj > ss)
    c[:, 384:512] = same & (jj >= ss)
    c[:, 512:640] = np.eye(128)
    c[:, 640:768] = same
    rm = np.ones(256, np.float32)
    rm[::64] = 0.0
    c[:, 768:1024] = rm[None, :]
    c[:, 1024:1152] = 1.0
    return c


E_ENTRIES = [(d, d + 1) for d in range(14)] + [(None, 3), (4, 5), (6, 7), (8, 9), (10, None)]


def _etables(rpb):
    Lr = rpb.shape[0]
    kc = np.arange(64)[:, None]
    qc = np.arange(64)[None, :]
    cs = np.clip(qc - 8, 0, 48)
    win = (kc >= cs) & (kc < cs + 16)
    dc = np.clip(kc - qc + 15, 0, 30)
    out = np.zeros((Lr, 4, 128, NE, 128), np.float32)
    mask = np.zeros((128, NE, 128), np.float32)
    for ei, pair in enumerate(E_ENTRIES):
        for half in range(2):
            dr = pair[half]
            if dr is None:
                continue
            for h in range(2):
                mask[half * 64:(half + 1) * 64, ei, h * 64:(h + 1) * 64] = win
                for hp in range(4):
                    out[:, hp, half * 64:(half + 1) * 64, ei, h * 64:(h + 1) * 64] = rpb[:, 2 * hp + h, dr][:, dc]
    return out.reshape(Lr, 4, 128, NE * 128), mask.reshape(128, NE * 128)


def _pack(inp):
    f = lambda a: np.ascontiguousarray(np.asarray(a, dtype=np.float32))
    pcol = np.zeros((128, L * PC_PER_L), np.float32)
    for l in range(L):
        b = l * PC_PER_L
        pcol[:, b:b + 14] = f(inp["shift_mu"])[l].reshape(14, 128).T
        pcol[:, b + 14:b + 22] = f(inp["w0"])[l].reshape(8, 128).T
        pcol[:, b + 22:b + 30] = f(inp["a0"])[l].reshape(8, 128).T
        pcol[:, b + 30:b + 34] = f(inp["k_k"])[l].reshape(4, 128).T
        pcol[:, b + 34:b + 38] = f(inp["k_a"])[l].reshape(4, 128).T
        pcol[:, b + 38:b + 42] = f(inp["r_k"])[l].reshape(4, 128).T
        pcol[:, b + 42:b + 46] = f(inp["gn_w"])[l].reshape(4, 128).T
        pcol[:, b + 46:b + 50] = f(inp["gn_b"])[l].reshape(4, 128).T
    rpbT, emask = _etables(f(inp["rpb"]))
    shared = {
        "w_in": f(inp["w_in"]), "w_pa": f(inp["w_pa"]), "w_pb": f(inp["w_pb"]), "w_out": f(inp["w_out"]),
        "gains": np.concatenate([f(inp["norm_g"]), f(inp["final_g"])[None, :]], 0),
        "pcol": pcol,
        "w2t": f(inp["w2"]).reshape(L, 128, 512), "a2t": f(inp["a2"]).reshape(L, 128, 512),
        "consts": _consts(), "emask": emask, "rpbT": rpbT,
    }
    return shared


_NC_CACHE = {}


def run(x_all, inp, T, NSEQ, ncores=NCORES):
    key = (T, NSEQ)
    if key not in _NC_CACHE:
        _NC_CACHE[key] = build_program(T, NSEQ)
    nc = _NC_CACHE[key]
    shared = _pack(inp)
    in_maps = []
    for c in range(ncores):
        m = dict(shared)
        m["x"] = np.ascontiguousarray(x_all[c * NSEQ:(c + 1) * NSEQ])
        in_maps.append(m)
    res = run_bass_kernel_spmd(nc, in_maps, core_ids=list(range(ncores)))
    return np.concatenate([r["y"] for r in res.results], 0), res


def kernel(x_prompt, x_sample, norm_g, w_in, shift_mu, w0, w2, a0, a2, k_k, k_a, r_k,
           gn_w, gn_b, rpb, w_pa, w_pb, w_out, final_g):
    inp = dict(norm_g=norm_g, w_in=w_in, shift_mu=shift_mu, w0=w0, w2=w2, a0=a0, a2=a2, k_k=k_k, k_a=k_a,
               r_k=r_k, gn_w=gn_w, gn_b=gn_b, rpb=rpb, w_pa=w_pa, w_pb=w_pb, w_out=w_out, final_g=final_g)
    xp = np.asarray(x_prompt, np.float32)
    xs = np.asarray(x_sample, np.float32)
    x_all = np.concatenate([xp, xs], 0)
    y, _ = run(x_all, inp, 2048, 3)
    return y[:xp.shape[0]].astype(np.float32), y[xp.shape[0]:].astype(np.float32)
```

```python
import numpy as np
import concourse.bass as bass
import concourse.mybir as mybir
from concourse.bass_utils import run_bass_kernel_spmd

F32 = mybir.dt.float32
BF16 = mybir.dt.bfloat16
AF = mybir.ActivationFunctionType
ALU = mybir.AluOpType
AX = mybir.AxisListType

D = 1024
DIN = 6400
L = 2
NCORES = 8
OFF_GA = 1792
OFF_QB = OFF_GA + 512
OFF_KB = OFF_QB + 512
OFF_VB = OFF_KB + 512
OFF_GB = OFF_VB + 512
OFF_MA = OFF_GB + 512
OFF_MB = OFF_MA + 1024
RMS_EPS = 1e-6
GN_EPS = 6.4e-4
CDEC = float(np.exp(-0.5))
NE = 19
PC_PER_L = 50


class Tok:
    __slots__ = ("w", "r", "name")

    def __init__(self, name=""):
        self.w = None
        self.r = {}
        self.name = name


class Prog:
    ENGS = ("pe", "act", "dve", "pool", "sp")

    def __init__(self):
        self.ops = {e: [] for e in self.ENGS}
        self.cnt = {e: 0 for e in ("pe", "act", "dve", "pool")}
        self.seen = {e: {} for e in self.ENGS}
        self.dcnt = {}
        self.pending = {e: {} for e in self.ENGS}

    def _need(self, eng, waits, key, val, raw, is_dma=False):
        if key == eng and not is_dma and (eng == "pe" or not raw):
            return
        if self.seen[eng].get(key, 0) >= val:
            return
        if waits.get(key, 0) < val:
            waits[key] = val

    def _deps(self, eng, reads, writes, is_dma=False):
        waits = dict(self.pending[eng])
        self.pending[eng] = {}
        for t in reads:
            if t.w is not None:
                self._need(eng, waits, t.w[0], t.w[1], True, is_dma)
        for t in writes:
            if t.w is not None:
                self._need(eng, waits, t.w[0], t.w[1], False, is_dma)
            for k, v in t.r.items():
                self._need(eng, waits, k, v, False, is_dma)
        for k, v in waits.items():
            if self.seen[eng].get(k, 0) < v:
                self.seen[eng][k] = v
        return waits

    def op(self, eng, fn, reads=(), writes=()):
        waits = self._deps(eng, reads, writes)
        self.cnt[eng] += 1
        c = self.cnt[eng]
        for t in reads:
            if t.r.get(eng, 0) < c:
                t.r[eng] = c
        for t in writes:
            t.w = (eng, c)
            t.r = {}
        self.ops[eng].append((waits, fn, eng, 1))

    def dma(self, queue, dsem, fn, reads=(), writes=()):
        waits = self._deps(queue, reads, writes, True)
        self.dcnt[dsem] = self.dcnt.get(dsem, 0) + 16
        c = self.dcnt[dsem]
        for t in reads:
            if t.r.get(dsem, 0) < c:
                t.r[dsem] = c
        for t in writes:
            t.w = (dsem, c)
            t.r = {}
        self.ops[queue].append((waits, fn, dsem, 16))

    def barrier(self):
        snap = dict(self.cnt)
        snap.update(self.dcnt)
        for e in self.ENGS:
            for k, v in snap.items():
                if k == e:
                    continue
                if self.seen[e].get(k, 0) < v and self.pending[e].get(k, 0) < v:
                    self.pending[e][k] = v

    def final_wait(self, eng="sp"):
        waits = {}
        for k, v in self.dcnt.items():
            waits[k] = v
        for k, v in self.cnt.items():
            waits[k] = v
        self.ops[eng].append((waits, None, None, 0))


def build_program(T, NSEQ):
    assert T % 512 == 0
    NT = T // 128
    NST = T // 256
    ROWS = T // 64
    nc = bass.Bass("TRN2", target_bir_lowering=False)
    P = Prog()

    x_d = nc.dram_tensor("x", [NSEQ, T, D], F32, kind="ExternalInput").ap()
    y_d = nc.dram_tensor("y", [NSEQ, T, D], F32, kind="ExternalOutput").ap()
    x1_d = nc.dram_tensor("x1s", [NSEQ, T, D], F32, kind="ExternalOutput").ap()
    win_d = nc.dram_tensor("w_in", [L, D, DIN], F32, kind="ExternalInput").ap()
    wpa_d = nc.dram_tensor("w_pa", [L, 512, D], F32, kind="ExternalInput").ap()
    wpb_d = nc.dram_tensor("w_pb", [L, 512, D], F32, kind="ExternalInput").ap()
    wout_d = nc.dram_tensor("w_out", [L, D, D], F32, kind="ExternalInput").ap()
    g_d = nc.dram_tensor("gains", [3, D], F32, kind="ExternalInput").ap()
    pcol_d = nc.dram_tensor("pcol", [128, L * PC_PER_L], F32, kind="ExternalInput").ap()
    w2_d = nc.dram_tensor("w2t", [L, 128, 512], F32, kind="ExternalInput").ap()
    a2_d = nc.dram_tensor("a2t", [L, 128, 512], F32, kind="ExternalInput").ap()
    cst_d = nc.dram_tensor("consts", [128, 1280], F32, kind="ExternalInput").ap()
    emask_d = nc.dram_tensor("emask", [128, NE * 128], F32, kind="ExternalInput").ap()
    rpb_d = nc.dram_tensor("rpbT", [L, 4, 128, NE * 128], F32, kind="ExternalInput").ap()

    class Bump:
        def __init__(self, start):
            self.off = start

        def alloc(self, name, shape, dt):
            nbytes = int(np.prod(shape[1:])) * (4 if dt == F32 else 2)
            nbytes = (nbytes + 31) // 32 * 32
            t = nc.alloc_sbuf_tensor_at(name, list(shape), dt, offset=self.off)
            self.off += nbytes
            return t

    pers = Bump(17408)
    hT = pers.alloc("hT", [128, 8, T + 2], BF16)
    yaT = pers.alloc("yaT", [128, 4, T], BF16)
    ybT = pers.alloc("ybT", [128, 4, T], BF16)
    NW = 5
    wslot = [pers.alloc(f"ws{i}", [128, 8, 128], BF16) for i in range(NW)]
    cmask = pers.alloc("cmask", [128, 4, 128], BF16)
    ident_b = pers.alloc("ident_b", [128, 128], BF16)
    ident_f = pers.alloc("ident_f", [128, 128], F32)
    bones_f = pers.alloc("bones_f", [128, 128], F32)
    ones_b = pers.alloc("ones_b", [128, 128], BF16)
    rmask = pers.alloc("rmask", [128, 256], F32)
    pcol = pers.alloc("pcol", [128, L * PC_PER_L], F32)
    pc2 = pers.alloc("pc2", [128, L * 28], F32)
    _w2 = [pers.alloc(f"w2t{d_}", [128, 512], BF16) for d_ in range(2)]
    _a2 = [pers.alloc(f"a2t{d_}", [128, 512], BF16) for d_ in range(2)]
    w2t = [_w2 for l in range(L)]
    a2t = [_a2 for l in range(L)]
    PH0 = pers.off

    phN = Bump(PH0)
    mergedT = phN.alloc("mergedT", [128, 8, T], BF16)
    woutT = phN.alloc("woutT", [128, 8, D], BF16)
    xst = [phN.alloc(f"xst{i}", [128, D], F32) for i in range(2)]
    xnew = [phN.alloc(f"xnew{i}", [128, D], F32) for i in range(2)]
    yst = [phN.alloc(f"yst{i}", [128, D], F32) for i in range(2)]
    junk = phN.alloc("junk", [128, D], BF16)
    htok = [phN.alloc(f"htok{i}", [128, D], BF16) for i in range(2)]
    stat = phN.alloc("stat", [128, 16], F32)
    msg = [phN.alloc("msg0", [128, 4, 512], F32)] * 2
    grep = [phN.alloc(f"grep{i}", [128, D], F32) for i in range(2)]
    cst_f = phN.alloc("cst_f", [128, 1280], F32)
    endN = phN.off

    phH = Bump(PH0)
    loraW = phH.alloc("loraW", [128, T], BF16)
    loraA = phH.alloc("loraA", [128, T], BF16)
    sga = phH.alloc("sga", [128, T], BF16)
    zs = [phH.alloc(f"zs{i}", [128, 258], F32) for i in range(2)]
    HP0 = phH.off
    phA = Bump(HP0)
    qT = phA.alloc("qT", [128, T // 64, 128], BF16)
    kT = phA.alloc("kT", [128, T], BF16)
    sgb = phA.alloc("sgb", [128, T], BF16)
    vbt = phA.alloc("vbt", [128, NT, 128], BF16)
    rpbb = phA.alloc("rpbb", [128, NE, 128], BF16)
    etf = phA.alloc("etf", [128, NE, 128], F32)
    etab = phA.alloc("etab", [128, NE, 128], BF16)
    emask = phA.alloc("emask", [128, NE, 128], BF16)
    exb = [phA.alloc(f"exb{i}", [128, 5, 128], BF16) for i in range(3)]
    ptb = [phA.alloc(f"ptb{i}", [128, 5, 128], BF16) for i in range(3)]
    rden = [phA.alloc(f"rden{i}", [128, 64], F32) for i in range(3)]
    g2 = [phA.alloc(f"g2{i}", [128, 64], F32) for i in range(3)]
    endA = phA.off
    phR = Bump(HP0)
    o0T = phR.alloc("o0T", [128, T], F32)
    bon0 = phR.alloc("bon0", [128, T], BF16)
    FN = ("rs", "ks", "vs", "kkn", "sig", "aa", "fc", "uu", "ww", "inc", "e1", "e2", "e3", "e4", "kdir")
    TB = []
    for d_ in range(2):
        B = {}
        for nm in FN:
            B[nm] = phR.alloc(f"f{d_}_" + nm, [128, 256], F32)
        B["wc"] = [phR.alloc(f"wc{d_}{q}", [128, 4], F32) for q in range(2)]
        B["bon"] = [phR.alloc(f"bon{d_}{q}", [128, 256], BF16) for q in range(2)]
        B["AR"] = [phR.alloc(f"AR{d_}{q}", [128, 4, 2, 128], BF16) for q in range(2)]
        for nm in ("KT", "BT", "KH", "BH", "VT"):
            B[nm] = [phR.alloc(f"{nm}{d_}{q}", [128, 4, 128], BF16) for q in range(2)]
        for nm in ("Atok", "KHtok", "Vtok", "AakT", "ArkT"):
            B[nm] = phR.alloc(f"{nm}{d_}", [128, 4, 128], BF16)
        B["RB"] = phR.alloc(f"RB{d_}", [128, 4, 2, 128], BF16)
        B["QP"] = [phR.alloc(f"QP{d_}{i}", [128, 4, 2, 128], BF16) for i in range(2)]
        B["Qb"] = [phR.alloc(f"Qb{d_}{i}", [128, 4, 128], BF16) for i in range(2)]
        B["GT"] = phR.alloc(f"GT{d_}", [128, 4, 128], F32)
        B["ObT"] = phR.alloc(f"ObT{d_}", [128, 256], F32)
        B["Hs"] = phR.alloc(f"Hs{d_}", [128, 4, 128], F32)
        B["Sst"] = phR.alloc(f"Sst{d_}", [128, 128], F32)
        B["Sbf"] = phR.alloc(f"Sbf{d_}", [128, 128], BF16)
        TB.append(B)
    endR = phR.off
    top = max(endN, endA, endR)
    assert top <= 229376, (PH0, endN, endA, endR)

    banks = [nc.alloc_psum_tensor(f"pb{i}", [128, 512], F32) for i in range(8)]
    bank_tok = [Tok(f"bank{i}") for i in range(8)]
    rr = [0]

    def bk(i):
        return banks[i][:, :]

    def nextbank(n=6):
        i = rr[0] % n
        rr[0] += 1
        return i

    tk = {}

    def T_(name):
        if name not in tk:
            tk[name] = Tok(name)
        return tk[name]

    def mm(out, lhsT, rhs, start, stop, reads, writes):
        P.op("pe", lambda e, o=out, l=lhsT, r=rhs, s=start, t=stop: e.matmul(o, lhsT=l, rhs=r, start=s, stop=t),
             reads, writes)

    def tr(out, in_, reads, writes):
        P.op("pe", lambda e, o=out, i=in_: e.transpose(out=o, in_=i, identity=ident_b[:, :]),
             reads + [T_("ident_b")], writes)

    def act(out, in_, func, reads, writes, bias=None, scale=None):
        kw = {}
        if bias is not None:
            kw["bias"] = bias
        if scale is not None:
            kw["scale"] = scale
        P.op("act", lambda e, o=out, i=in_, f=func, kw=kw: e.activation(out=o, in_=i, func=f, **kw), reads, writes)

    def tt(eng, out, in0, in1, op, reads, writes):
        P.op(eng, lambda e, o=out, a=in0, b=in1, p=op: e.tensor_tensor(out=o, in0=a, in1=b, op=p), reads, writes)

    def ts(eng, out, in0, s1, s2, op0, op1, reads, writes):
        if op1 is None:
            P.op(eng, lambda e, o=out, a=in0, s=s1, p=op0: e.tensor_scalar(out=o, in0=a, scalar1=s, scalar2=None, op0=p),
                 reads, writes)
        else:
            P.op(eng, lambda e, o=out, a=in0, s=s1, q=s2, p=op0, r=op1:
                 e.tensor_scalar(out=o, in0=a, scalar1=s, scalar2=q, op0=p, op1=r), reads, writes)

    def stt(out, in0, scalar, in1, op0, op1, reads, writes):
        P.op("dve", lambda e, o=out, a=in0, s=scalar, b=in1, p=op0, q=op1:
             e.scalar_tensor_tensor(out=o, in0=a, scalar=s, in1=b, op0=p, op1=q), reads, writes)

    def cp(eng, out, in_, reads, writes):
        if eng == "act":
            P.op("act", lambda e, o=out, i=in_: e.activation(out=o, in_=i, func=AF.Copy), reads, writes)
        else:
            P.op(eng, lambda e, o=out, i=in_: e.tensor_copy(out=o, in_=i), reads, writes)

    def recip(out, in_, reads, writes):
        P.op("dve", lambda e, o=out, i=in_: e.reciprocal(out=o, in_=i), reads, writes)

    def dma(queue, dsem, out, in_, reads, writes):
        P.dma(queue, dsem, lambda e, o=out, i=in_: e.dma_start(out=o, in_=i), reads, writes)

    t_c = T_("consts")
    dma("sp", "d_c0", cst_f[:, :], cst_d[:, :], [], [T_("cst_f")])
    dma("sp", "d_c1", pcol[:, :], pcol_d[:, :], [], [T_("pcol")])
    for d_ in range(2):
        P.op("pool", lambda e, t=_w2[d_]: e.memset(t[:, :], 0.0), [], [T_("w2t")])
        P.op("pool", lambda e, t=_a2[d_]: e.memset(t[:, :], 0.0), [], [T_("w2t")])

    def load_lora2(l):
        for d_ in range(2):
            hs_ = slice(d_ * 64, (d_ + 1) * 64)
            dma("pool", "d_c3", _w2[d_][hs_, :], w2_d[l][hs_, :], [], [T_("w2t")])
            dma("pool", "d_c3", _a2[d_][hs_, :], a2_d[l][hs_, :], [], [T_("w2t")])

    cp("dve", cmask[:, :, :], cst_f[:, 0:512].rearrange("p (a b) -> p a b", b=128), [T_("cst_f")], [T_("cmask")])
    cp("dve", ident_b[:, :], cst_f[:, 512:640], [T_("cst_f")], [T_("ident_b")])
    cp("dve", ident_f[:, :], cst_f[:, 512:640], [T_("cst_f")], [T_("ident_f")])
    cp("dve", bones_f[:, :], cst_f[:, 640:768], [T_("cst_f")], [T_("bones_f")])
    cp("dve", rmask[:, :], cst_f[:, 768:1024], [T_("cst_f")], [T_("rmask")])
    cp("dve", ones_b[:, :], cst_f[:, 1024:1152], [T_("cst_f")], [T_("ones_b")])
    for l in range(L):
        ts("dve", pc2[:, l * 28:l * 28 + 14], pcol[:, l * PC_PER_L:l * PC_PER_L + 14], -1.0, 1.0, ALU.mult, ALU.add,
           [T_("pcol")], [T_("pc2")])
        ts("dve", pc2[:, l * 28 + 14:l * 28 + 28], pcol[:, l * PC_PER_L:l * PC_PER_L + 14], 0.5, None, ALU.mult, None,
           [T_("pcol")], [T_("pc2")])
    P.op("pool", lambda e: e.memset(hT[:, :, 0:1], 0.0), [], [T_("hTpad")])
    P.op("pool", lambda e: e.memset(hT[:, :, T + 1:T + 2], 0.0), [], [T_("hTpad")])

    def pc(l, name, idx=0):
        base = l * PC_PER_L
        offs = {"mu": 0, "w0": 14, "a0": 22, "k_k": 30, "k_a": 34, "r_k": 38, "gn_w": 42, "gn_b": 46}
        c = base + offs[name] + idx
        return pcol[:, c:c + 1]

    wtok = [Tok(f"ws{i}") for i in range(NW)]
    wrr = [0]

    def load_w(src_ap, kcs):
        i = wrr[0] % NW
        wrr[0] += 1
        dma("pool", f"d_w{i}", wslot[i][:, 0:kcs, :], src_ap.rearrange("(k p) c -> p k c", p=128), [], [wtok[i]])
        return i

    def load_g(which, slot):
        dma("sp", f"d_g{slot}", grep[slot][:, :], g_d[which, :].partition_broadcast(128), [], [T_(f"grep{slot}")])

    def norm_tile(src, src_tok, i, gslot, eps, out_h=None, out_f=None, out_tok=None):
        k = i % 2
        if i == 2:
            chk(109)
        chk(100)
        act(junk[:, :], src, AF.Square, [src_tok], [T_("junk")])
        chk(101)
        P.op("dve", lambda e, i=i: e.tensor_reduce(out=stat[:, i % 16:i % 16 + 1], in_=junk[:, :], axis=AX.X, op=ALU.add),
             [T_("junk")], [T_(f"stat{i % 16}")])
        chk(102)
        act(stat[:, i % 16:i % 16 + 1], stat[:, i % 16:i % 16 + 1], AF.Sqrt, [T_(f"stat{i % 16}")], [T_(f"stat{i % 16}")],
            bias=float(eps), scale=1.0 / D)
        chk(103)
        recip(stat[:, i % 16:i % 16 + 1], stat[:, i % 16:i % 16 + 1], [T_(f"stat{i % 16}")], [T_(f"stat{i % 16}")])
        chk(104)
        if out_f is not None:
            stt(out_f, src, stat[:, i % 16:i % 16 + 1], grep[gslot][:, :], ALU.mult, ALU.mult,
                [src_tok, T_(f"stat{i % 16}"), T_(f"grep{gslot}")], [out_tok])
            return
        stt(htok[k][:, :], src, stat[:, i % 16:i % 16 + 1], grep[gslot][:, :], ALU.mult, ALU.mult,
            [src_tok, T_(f"stat{i % 16}"), T_(f"grep{gslot}")], [T_(f"htok{k}")])
        chk(105)
        b = nextbank()
        pb = bk(b).bitcast(BF16)
        for kc in range(8):
            tr(pb[:, kc * 128:(kc + 1) * 128], htok[k][:, kc * 128:(kc + 1) * 128], [T_(f"htok{k}")], [bank_tok[b]])
        chk(106)
        cp("act", hT[:, :, 1 + i * 128:1 + (i + 1) * 128], pb[:, 0:1024].rearrange("p (k t) -> p k t", t=128),
           [bank_tok[b]], [T_("hT")])
        chk(107)
        if i == 1:
            chk(108)
        if i == 2:
            chk(110)
        if i == 3:
            chk(111)

    def zmm(b, ncols, wi, kcs, rhs_fn, rhs_tok, col0=0, extra=()):
        for kc in range(kcs):
            mm(bk(b)[:, col0:col0 + ncols], wslot[wi][:, kc, :], rhs_fn(kc), kc == 0, kc == kcs - 1,
               [wtok[wi], rhs_tok] + list(extra), [bank_tok[b]])

    def shifted_block(l, wi, mublk, t0, out_ap, out_tok, k):
        b = nextbank()
        zmm(b, 258, wi, 8, lambda kc: hT[:, kc, t0:t0 + 258], T_("hT"), extra=[T_("hTpad")])
        cp("act", zs[k][:, :], bk(b)[:, 0:258], [bank_tok[b]], [T_(f"zs{k}")])
        tt("dve", TB[k]["uu"][:, :], zs[k][:, 0:256], zs[k][:, 2:258], ALU.add, [T_(f"zs{k}")], [T_(f"f{k}_uu")])
        ts("dve", TB[k]["uu"][:, :], TB[k]["uu"][:, :], pc2[:, l * 28 + 14 + mublk:l * 28 + 15 + mublk], None, ALU.mult, None,
           [T_(f"f{k}_uu"), T_("pc2")], [T_(f"f{k}_uu")])
        stt(out_ap, zs[k][:, 1:257], pc2[:, l * 28 + mublk:l * 28 + mublk + 1], TB[k]["uu"][:, :], ALU.mult, ALU.add,
            [T_(f"zs{k}"), T_(f"f{k}_uu"), T_("pc2")], [out_tok])

    HB = [slice(0, 64), slice(64, 128)]
    import os as _os
    KSTOP = int(_os.environ.get("KSTOP", "99"))

    class _Stop(Exception):
        pass

    def chk(stage):
        if stage == KSTOP:
            raise _Stop()

    def body():

        for s in range(NSEQ):
            for l in range(L):
                if l == 0:
                    P.barrier()
                    load_g(0, 0)
                    for i in range(NT):
                        k = i % 2
                        dma("sp", f"d_x{k}", xst[k][:, :], x_d[s, i * 128:(i + 1) * 128, :], [], [T_(f"xst{k}")])
                        norm_tile(xst[k][:, :], T_(f"xst{k}"), i, 0, RMS_EPS)
                P.barrier()
                w_l = win_d[l]
                load_lora2(l)
                for j in range(2):
                    wi = load_w(w_l[:, 1536 + j * 128:1536 + (j + 1) * 128], 8)
                    for st in range(NST):
                        k = st % 2
                        shifted_block(l, wi, 12 + j, st * 256, TB[k]["e1"][:, :], T_(f"f{k}_e1"), k)
                        if j == 0:
                            act(loraW[:, st * 256:(st + 1) * 256], TB[k]["e1"][:, :], AF.Tanh, [T_(f"f{k}_e1")], [T_("loraW")])
                        else:
                            cp("act", loraA[:, st * 256:(st + 1) * 256], TB[k]["e1"][:, :], [T_(f"f{k}_e1")], [T_("loraA")])
                chk(2)
                for hp in range(4):
                    c0 = hp * 128
                    P.barrier()
                    dma("pool", "d_c2", emask[:, :, :], emask_d.rearrange("p (e c) -> p e c", c=128), [], [T_("emask")])
                    dma("pool", "d_rpb", rpbb[:, :, :], rpb_d[l, hp].rearrange("p (e c) -> p e c", c=128), [], [T_("rpbb")])
                    act(etf[:, :, :], rpbb[:, :, :], AF.Exp, [T_("rpbb")], [T_("etf")])
                    tt("dve", etab[:, :, :], etf[:, :, :], emask[:, :, :], ALU.mult, [T_("etf"), T_("emask")], [T_("etab")])
                    P.op("pool", lambda e: e.memset(qT[:, :, :], 0.0), [], [T_("qT")])
                    wq = load_w(w_l[:, OFF_QB + c0:OFF_QB + c0 + 128], 8)
                    wk = load_w(w_l[:, OFF_KB + c0:OFF_KB + c0 + 128], 8)
                    wv = load_w(w_l[:, OFF_VB + c0:OFF_VB + c0 + 128], 8)
                    wgb = load_w(w_l[:, OFF_GB + c0:OFF_GB + c0 + 128], 8)
                    wga = load_w(w_l[:, OFF_GA + c0:OFF_GA + c0 + 128], 8)
                    for t4 in range(T // 512):
                        tsl = slice(t4 * 512, (t4 + 1) * 512)
                        hsl = slice(1 + t4 * 512, 1 + (t4 + 1) * 512)
                        b = nextbank()
                        zmm(b, 512, wq, 8, lambda kc, hsl=hsl: hT[:, kc, hsl], T_("hT"))
                        for h in range(2):
                            act(qT[HB[h], t4 * 8:(t4 + 1) * 8, h * 64:(h + 1) * 64],
                                bk(b)[HB[h], :].rearrange("p (r q) -> p r q", q=64), AF.Copy,
                                [bank_tok[b]], [T_("qT")], scale=0.125)
                        b = nextbank()
                        zmm(b, 512, wk, 8, lambda kc, hsl=hsl: hT[:, kc, hsl], T_("hT"))
                        cp("dve", kT[:, tsl], bk(b)[:, :], [bank_tok[b]], [T_("kT")])
                        b = nextbank()
                        zmm(b, 512, wgb, 8, lambda kc, hsl=hsl: hT[:, kc, hsl], T_("hT"))
                        act(sgb[:, tsl], bk(b)[:, :], AF.Silu, [bank_tok[b]], [T_("sgb")])
                        b = nextbank()
                        zmm(b, 512, wga, 8, lambda kc, hsl=hsl: hT[:, kc, hsl], T_("hT"))
                        act(sga[:, tsl], bk(b)[:, :], AF.Silu, [bank_tok[b]], [T_("sga")])
                        b = nextbank()
                        for q4 in range(4):
                            i = t4 * 4 + q4
                            for kc in range(8):
                                mm(bk(b)[:, q4 * 128:(q4 + 1) * 128], hT[:, kc, 1 + i * 128:1 + (i + 1) * 128],
                                   wslot[wv][:, kc, :], kc == 0, kc == 7, [T_("hT"), wtok[wv]], [bank_tok[b]])
                        cp("dve", vbt[:, t4 * 4:(t4 + 1) * 4, :], bk(b)[:, :].rearrange("p (a c) -> p a c", c=128),
                           [bank_tok[b]], [T_("vbt")])
                    chk(3)
                    NB_ROW = 3

                    def row_front(r):
                        k = r % NB_ROW
                        rs_ = min(max(r - 4, 0), ROWS - 8)
                        if rs_ % 2 == 0:
                            nt_, tile0, e0, estep = 4, rs_ // 2, rs_ - r + 7, 2
                        else:
                            nt_, tile0, e0, estep = 5, (rs_ - 1) // 2, 14, 1
                        bA = nextbank()
                        bB = nextbank() if nt_ == 5 else None
                        for i in range(nt_):
                            bb_, col = (bA, i * 128) if i < 4 else (bB, 0)
                            mm(bk(bb_)[:, col:col + 128], kT[:, (tile0 + i) * 128:(tile0 + i + 1) * 128], qT[:, r, :],
                               True, True, [T_("kT"), T_("qT")], [bank_tok[bb_]])
                        act(exb[k][:, 0:4, :], bk(bA)[:, :].rearrange("p (a c) -> p a c", c=128), AF.Exp,
                            [bank_tok[bA]], [T_(f"exb{k}")])
                        if nt_ == 5:
                            act(exb[k][:, 4, :], bk(bB)[:, 0:128], AF.Exp, [bank_tok[bB]], [T_(f"exb{k}")])
                        tt("dve", ptb[k][:, 0:nt_, :], exb[k][:, 0:nt_, :], etab[:, e0:e0 + estep * (nt_ - 1) + 1:estep, :],
                           ALU.mult, [T_(f"exb{k}"), T_("etab")], [T_(f"ptb{k}")])
                        return nt_, tile0

                    def row_back(r, nt_, tile0):
                        k = r % NB_ROW
                        bO = nextbank()
                        for i in range(nt_):
                            mm(bk(bO)[:, 0:128], vbt[:, tile0 + i, :], ptb[k][:, i, :], i == 0, i == nt_ - 1,
                               [T_("vbt"), T_(f"ptb{k}")], [bank_tok[bO]])
                        for i in range(nt_):
                            mm(bk(bO)[:, 128:256], ones_b[:, :], ptb[k][:, i, :], i == 0, i == nt_ - 1,
                               [T_("ones_b"), T_(f"ptb{k}")], [bank_tok[bO]])
                        for h in range(2):
                            recip(rden[k][HB[h], :], bk(bO)[HB[h], 128 + h * 64:128 + (h + 1) * 64],
                                  [bank_tok[bO]], [T_(f"rden{k}")])
                        tt("dve", g2[k][:, :], rden[k][:, :], sgb[:, r * 64:(r + 1) * 64], ALU.mult,
                           [T_(f"rden{k}"), T_("sgb")], [T_(f"g2{k}")])
                        for h in range(2):
                            tt("dve", ybT[HB[h], hp, r * 64:(r + 1) * 64], bk(bO)[HB[h], h * 64:(h + 1) * 64],
                               g2[k][HB[h], :], ALU.mult, [bank_tok[bO], T_(f"g2{k}")], [T_("ybT")])

                    LOOK = 2
                    fr = {}
                    for r in range(min(LOOK, ROWS)):
                        fr[r] = row_front(r)
                    for r in range(ROWS):
                        if r + LOOK < ROWS:
                            fr[r + LOOK] = row_front(r + LOOK)
                        row_back(r, *fr.pop(r))
                    chk(4)
                    P.barrier()
                    wr = load_w(w_l[:, c0:c0 + 128], 8)
                    wkk = load_w(w_l[:, 512 + c0:512 + c0 + 128], 8)
                    wvv = load_w(w_l[:, 1024 + c0:1024 + c0 + 128], 8)

                    def v3(ap):
                        return ap.rearrange("p (c k) -> p c k", k=64)

                    def pb3(b, lo=0, hi=512):
                        return bk(b)[:, lo:hi].rearrange("p (c k) -> p c k", k=128)

                    SY = {"prep": [0, 0], "main": [0, 0], "out": set()}

                    def prep_gen(d, l=l, hp=hp, c0=c0, wr=wr, wkk=wkk, wvv=wvv):
                        B = TB[d]

                        def f(n):
                            return B[n][:, :]

                        def ft(n):
                            return T_(f"f{d}_{n}")

                        for q in range(2):
                            for nm in ("AR", "KT", "BT", "KH", "BH", "VT"):
                                buf = B[nm][q]
                                P.op("pool", lambda e, buf=buf: e.memset(buf[tuple(slice(None) for _ in buf.shape)], 0.0), [],
                                     [T_(f"{nm}{q}_{d}")])
                        st_order = list(range(NST)) if d == 0 else list(range(NST - 1, -1, -1))
                        for idx, st in enumerate(st_order):
                            q = idx % 2
                            while SY["main"][d] < idx - 1:
                                yield 0
                            AR, KT, BT, KH, BH, VT, wc = B["AR"][q], B["KT"][q], B["BT"][q], B["KH"][q], B["BH"][q], B["VT"][q], B["wc"][q]

                            def X(n):
                                return T_(f"{n}{q}_{d}")

                            first = (st < NST // 2) if d == 0 else (st >= NST // 2)
                            t0 = st * 256
                            tsl = slice(t0, t0 + 256)
                            shifted_block(l, wr, hp, t0, f("rs"), ft("rs"), d)
                            yield 0
                            shifted_block(l, wkk, 4 + hp, t0, f("ks"), ft("ks"), d)
                            yield 0
                            shifted_block(l, wvv, 8 + hp, t0, f("vs"), ft("vs"), d)
                            yield 0
                            ts("dve", f("uu"), f("ks"), pc(l, "k_k", hp), None, ALU.mult, None, [ft("ks"), T_("pcol")], [ft("uu")])
                            tt("dve", f("ww"), f("uu"), f("uu"), ALU.mult, [ft("uu")], [ft("ww")])
                            b = nextbank(6)
                            mm(bk(b)[:, 0:256], bones_f[:, :], f("ww"), True, True, [T_("bones_f"), ft("ww")], [bank_tok[b]])
                            b2_ = nextbank(6)
                            mm(bk(b2_)[:, 0:256], w2t[l][d][:, c0:c0 + 128], loraW[:, tsl], True, True,
                               [T_("w2t"), T_("loraW")], [bank_tok[b2_]])
                            mm(bk(b2_)[:, 256:512], a2t[l][d][:, c0:c0 + 128], loraA[:, tsl], True, True,
                               [T_("w2t"), T_("loraA")], [bank_tok[b2_]])
                            act(f("sig"), bk(b2_)[:, 0:256], AF.Sigmoid, [bank_tok[b2_], T_("pcol")], [ft("sig")],
                                bias=pc(l, "w0", d * 4 + hp))
                            act(f("aa"), bk(b2_)[:, 256:512], AF.Sigmoid, [bank_tok[b2_], T_("pcol")], [ft("aa")],
                                bias=pc(l, "a0", d * 4 + hp))
                            act(f("e1"), bk(b)[:, 0:256], AF.Ln, [bank_tok[b]], [ft("e1")], bias=1e-24)
                            act(f("e1"), f("e1"), AF.Exp, [ft("e1")], [ft("e1")], scale=-0.5)
                            yield 0
                            tt("dve", f("kkn"), f("uu"), f("e1"), ALU.mult, [ft("uu"), ft("e1")], [ft("kkn")])
                            P.op("dve", lambda e: e.tensor_tensor_scan(out=B["fc"][:, :], data0=rmask[:, :], data1=B["sig"][:, :],
                                                                      initial=0.0, op0=ALU.mult, op1=ALU.add),
                                 [T_("rmask"), ft("sig")], [ft("fc")])
                            yield 0
                            tt("dve", f("uu"), f("fc"), f("sig"), ALU.subtract, [ft("fc"), ft("sig")], [ft("uu")])
                            totb = v3(f("fc"))[:, :, 63:64].to_broadcast([128, 4, 64])
                            tt("dve", v3(f("ww")), totb, v3(f("fc")), ALU.subtract, [ft("fc")], [ft("ww")])
                            if d == 0:
                                inc, exc, exo = "fc", "uu", "ww"
                            else:
                                tt("dve", f("inc"), f("ww"), f("sig"), ALU.add, [ft("ww"), ft("sig")], [ft("inc")])
                                inc, exc, exo = "inc", "ww", "uu"
                            act(f("e1"), f(inc), AF.Exp, [ft(inc)], [ft("e1")], scale=-CDEC)
                            act(f("e2"), f(exc), AF.Exp, [ft(exc)], [ft("e2")], scale=-CDEC)
                            act(f("e3"), f(inc), AF.Exp, [ft(inc)], [ft("e3")], scale=CDEC)
                            act(f("e4"), f(exo), AF.Exp, [ft(exo)], [ft("e4")], scale=-CDEC)
                            act(wc[:, :], v3(f("fc"))[:, :, 63], AF.Exp, [ft("fc")], [X("wc")], scale=-CDEC)
                            yield 0
                            ts("dve", f("fc"), f("aa"), -1.0, pc(l, "k_a", hp), ALU.add, ALU.mult, [ft("aa"), T_("pcol")], [ft("fc")])
                            stt(f("kdir"), f("fc"), 1.0, f("ks"), ALU.add, ALU.mult, [ft("fc"), ft("ks")], [ft("kdir")])
                            tt("dve", f("sig"), f("kkn"), f("aa"), ALU.mult, [ft("kkn"), ft("aa")], [ft("sig")])
                            stt(f("inc"), f("rs"), pc(l, "r_k", hp), f("kdir"), ALU.mult, ALU.mult,
                                [ft("rs"), ft("kdir"), T_("pcol")], [ft("inc")])
                            b = nextbank(6)
                            mm(bk(b)[:, 0:256], bones_f[:, :], f("inc"), True, True, [T_("bones_f"), ft("inc")], [bank_tok[b]])
                            if first:
                                tt("dve", bon0[:, tsl], bk(b)[:, 0:256], f("vs"), ALU.mult, [bank_tok[b], ft("vs")], [T_(f"bst{st}")])
                            else:
                                tt("dve", B["bon"][q][:, :], bk(b)[:, 0:256], f("vs"), ALU.mult, [bank_tok[b], ft("vs")], [X("bon")])
                            yield 0
                            for h in range(2):
                                hs = HB[h]
                                eng = "dve"

                                def dst(buf, h=h, hs=hs):
                                    return buf[hs, :, h * 64:(h + 1) * 64]

                                stt(AR[hs, :, 0, h * 64:(h + 1) * 64], v3(B["kkn"][hs, :]), -1.0, v3(B["e2"][hs, :]), ALU.mult, ALU.mult,
                                    [ft("kkn"), ft("e2")], [X("AR")])
                                tt(eng, AR[hs, :, 1, h * 64:(h + 1) * 64], v3(B["rs"][hs, :]), v3(B["e1"][hs, :]), ALU.mult,
                                   [ft("rs"), ft("e1")], [X("AR")])
                                tt(eng, dst(KT), v3(B["kdir"][hs, :]), v3(B["e3"][hs, :]), ALU.mult, [ft("kdir"), ft("e3")], [X("KT")])
                                tt(eng, dst(BT), v3(B["sig"][hs, :]), v3(B["e3"][hs, :]), ALU.mult, [ft("sig"), ft("e3")], [X("BT")])
                                tt(eng, dst(KH), v3(B["kdir"][hs, :]), v3(B["e4"][hs, :]), ALU.mult, [ft("kdir"), ft("e4")], [X("KH")])
                                tt(eng, dst(BH), v3(B["sig"][hs, :]), v3(B["e4"][hs, :]), ALU.mult, [ft("sig"), ft("e4")], [X("BH")])
                                cp(eng, dst(VT), v3(B["vs"][hs, :]), [ft("vs")], [X("VT")])
                            SY["prep"][d] = idx + 1
                            yield 0

                    def rwkv_gen(d, l=l, hp=hp, c0=c0):
                        B = TB[d]
                        Atok, KHtok, RB, Vtok, AakT, ArkT = B["Atok"], B["KHtok"], B["RB"], B["Vtok"], B["AakT"], B["ArkT"]
                        QP, Qb, GT, ObT, Hs, Sst, Sbf = B["QP"], B["Qb"], B["GT"], B["ObT"], B["Hs"], B["Sst"], B["Sbf"]

                        def XM(n):
                            return T_(f"{n}_{d}")

                        P.op("pool", lambda e: e.memset(Sst[:, :], 0.0), [], [XM("Sst")])
                        P.op("pool", lambda e: e.memset(Sbf[:, :], 0.0), [], [XM("Sbf")])
                        ms, mi, mso = (0, 1, 2) if d == 0 else (2, 3, 0)
                        st_order = list(range(NST)) if d == 0 else list(range(NST - 1, -1, -1))
                        for idx, st in enumerate(st_order):
                            q = idx % 2
                            while SY["prep"][d] <= idx:
                                yield 0
                            AR, KT, BT, KH, BH, VT, wc = B["AR"][q], B["KT"][q], B["BT"][q], B["KH"][q], B["BH"][q], B["VT"][q], B["wc"][q]
                            OPN = ("AR", "KT", "BT", "KH", "BH", "VT", "wc", "bon")

                            def X(n):
                                if n in OPN:
                                    return T_(f"{n}{q}_{d}")
                                return T_(f"{n}_{d}")

                            first = (st < NST // 2) if d == 0 else (st >= NST // 2)
                            t0 = st * 256
                            tsl = slice(t0, t0 + 256)

                            def mk(ix, n=4):
                                return cmask[:, ix, :].unsqueeze(1).to_broadcast([128, n, 128])

                            for src, srct, dstb, dstt in ((AR, "AR", Atok, "Atok"), (KH, "KH", KHtok, "KHtok"),
                                                          (BH, "BH", RB, "RBb"), (VT, "VT", Vtok, "Vtok")):
                                b = nextbank(6)
                                pbb = bk(b).bitcast(BF16)
                                for c in range(4):
                                    sap = src[:, c, 0, :] if src is AR else src[:, c, :]
                                    tr(pbb[:, c * 128:(c + 1) * 128], sap, [X(srct)], [bank_tok[b]])
                                dap = dstb[:, :, 1, :] if dstb is RB else dstb[:, :, :]
                                cp("act", dap, pbb[:, 0:512].rearrange("p (c k) -> p c k", k=128), [bank_tok[b]], [X(dstt)])
                            yield 0
                            b1 = nextbank(6); b2 = nextbank(6)
                            for c in range(4):
                                bb_ = b1 if c < 2 else b2
                                mm(bk(bb_)[:, (c % 2) * 256:(c % 2) * 256 + 256], KT[:, c, :],
                                   AR[:, c, :, :].rearrange("p a k -> p (a k)"), True, True, [X("KT"), X("AR")], [bank_tok[bb_]])
                            for bb_, cs_ in ((b1, slice(0, 2)), (b2, slice(2, 4))):
                                pv = bk(bb_)[:, :].rearrange("p (c a k) -> p c a k", a=2, k=128)
                                tt("dve", AakT[:, cs_, :], pv[:, :, 0, :], mk(ms, 2), ALU.mult, [bank_tok[bb_], T_("cmask")], [X("AakT")])
                                tt("dve", ArkT[:, cs_, :], pv[:, :, 1, :], mk(mi, 2), ALU.mult, [bank_tok[bb_], T_("cmask")], [X("ArkT")])
                            yield 0
                            b1 = nextbank(6); b2 = nextbank(6)
                            for c in range(4):
                                bb_ = b1 if c < 2 else b2
                                mm(bk(bb_)[:, (c % 2) * 256:(c % 2) * 256 + 256], BT[:, c, :],
                                   AR[:, c, :, :].rearrange("p a k -> p (a k)"), True, True, [X("BT"), X("AR")], [bank_tok[bb_]])
                            for bb_, cs_ in ((b1, slice(0, 2)), (b2, slice(2, 4))):
                                pv = bk(bb_)[:, :].rearrange("p (c a k) -> p c a k", a=2, k=128)
                                tt("dve", QP[0][:, cs_, 0, :], pv[:, :, 0, :], mk(ms, 2), ALU.mult, [bank_tok[bb_], T_("cmask")], [X("QP0")])
                                tt("dve", RB[:, cs_, 0, :], pv[:, :, 1, :], mk(mi, 2), ALU.mult, [bank_tok[bb_], T_("cmask")], [X("RBa")])
                            b = nextbank(6)
                            for c in range(4):
                                mm(bk(b)[:, c * 128:(c + 1) * 128], AR[:, c, 0, :], BT[:, c, :], True, True,
                                   [X("AR"), X("BT")], [bank_tok[b]])
                            tt("dve", Qb[0][:, :, :], pb3(b), mk(mso), ALU.mult, [bank_tok[b], T_("cmask")], [X("Qb0")])
                            yield 0
                            for lv in range(6):
                                ci, ni = lv % 2, (lv + 1) % 2
                                last = lv == 5
                                b1 = nextbank(6); b2 = nextbank(6)
                                for c in range(4):
                                    bb_ = b1 if c < 2 else b2
                                    if last:
                                        mm(bk(bb_)[:, (c % 2) * 256 + 128:(c % 2) * 256 + 256], Qb[ci][:, c, :],
                                           QP[ci][:, c, 1, :], True, True, [X(f"Qb{ci}"), X(f"QP{ci}")], [bank_tok[bb_]])
                                    elif lv == 0:
                                        mm(bk(bb_)[:, (c % 2) * 256:(c % 2) * 256 + 128], Qb[ci][:, c, :],
                                           QP[ci][:, c, 0, :], True, True, [X(f"Qb{ci}"), X(f"QP{ci}")], [bank_tok[bb_]])
                                    else:
                                        mm(bk(bb_)[:, (c % 2) * 256:(c % 2) * 256 + 256], Qb[ci][:, c, :],
                                           QP[ci][:, c, :, :].rearrange("p a k -> p (a k)"), True, True,
                                           [X(f"Qb{ci}"), X(f"QP{ci}")], [bank_tok[bb_]])
                                if not last:
                                    b = nextbank(6)
                                    for c in range(4):
                                        mm(bk(b)[:, c * 128:(c + 1) * 128], QP[ci][:, c, 0, :], Qb[ci][:, c, :], True, True,
                                           [X(f"QP{ci}"), X(f"Qb{ci}")], [bank_tok[b]])
                                for bb_, cs_ in ((b1, slice(0, 2)), (b2, slice(2, 4))):
                                    pv = bk(bb_)[:, :].rearrange("p (c a k) -> p c a k", a=2, k=128)
                                    if not last:
                                        cp("act", QP[ni][:, cs_, 0, :], pv[:, :, 0, :], [bank_tok[bb_]], [X(f"QP{ni}")])
                                    if lv == 0:
                                        tt("pool", QP[ni][:, cs_, 1, :], QP[ci][:, cs_, 0, :],
                                           ident_b[:, :].unsqueeze(1).to_broadcast([128, 2, 128]), ALU.add,
                                           [X(f"QP{ci}"), T_("ident_b")], [X(f"QP{ni}")])
                                    else:
                                        tt("dve", QP[ni][:, cs_, 1, :], pv[:, :, 1, :], QP[ci][:, cs_, 1, :], ALU.add,
                                           [bank_tok[bb_], X(f"QP{ci}")], [X(f"QP{ni}")])
                                if not last:
                                    cp("act", Qb[ni][:, :, :], pb3(b), [bank_tok[b]], [X(f"Qb{ni}")])
                                yield 0
                            MT = QP[0]
                            MTt = X("QP0")
                            Abar = QP[1][:, :, 0, :]
                            P1 = QP[1][:, :, 1, :]
                            Ubar = Qb[0]
                            RbT = Qb[1]
                            b = nextbank(6)
                            for c in range(4):
                                mm(bk(b)[:, c * 128:(c + 1) * 128], MT[:, c, 1, :], Atok[:, c, :], True, True,
                                   [MTt, X("Atok")], [bank_tok[b]])
                            bq = nextbank(6)
                            for c in range(4):
                                mm(bk(bq)[:, c * 128:(c + 1) * 128], AakT[:, c, :], Vtok[:, c, :], True, True,
                                   [X("AakT"), X("Vtok")], [bank_tok[bq]])
                            cp("act", Abar, pb3(b), [bank_tok[b]], [X("QP1")])
                            cp("dve", P1, pb3(bq), [bank_tok[bq]], [X("QP1")])
                            yield 0
                            b = nextbank(6)
                            for c in range(4):
                                mm(bk(b)[:, c * 128:(c + 1) * 128], MT[:, c, 1, :], QP[1][:, c, 1, :], True, True,
                                   [MTt, X("QP1")], [bank_tok[b]])
                            b1 = nextbank(6); b2 = nextbank(6)
                            for c in range(4):
                                bb_ = b1 if c < 2 else b2
                                mm(bk(bb_)[:, (c % 2) * 256:(c % 2) * 256 + 256], QP[1][:, c, 0, :],
                                   RB[:, c, :, :].rearrange("p a k -> p (a k)"), True, True,
                                   [X("QP1"), X("RBa"), X("RBb")], [bank_tok[bb_]])
                            cp("act", Ubar[:, :, :], pb3(b), [bank_tok[b]], [X("Qb0")])
                            for bb_, cs_ in ((b1, slice(0, 2)), (b2, slice(2, 4))):
                                pv = bk(bb_)[:, :].rearrange("p (c a k) -> p c a k", a=2, k=128)
                                tt("dve", RbT[:, cs_, :], pv[:, :, 0, :], AR[:, cs_, 1, :], ALU.add, [bank_tok[bb_], X("AR")], [X("Qb1")])
                                for c in (cs_.start, cs_.start + 1):
                                    stt(GT[:, c, :], ident_f[:, :], wc[:, c:c + 1], pv[:, c % 2, 1, :], ALU.mult, ALU.add,
                                        [bank_tok[bb_], T_("ident_f"), X("wc")], [X("GT")])
                            yield 0
                            b = nextbank(6)
                            for c in range(4):
                                mm(bk(b)[:, c * 128:(c + 1) * 128], Vtok[:, c, :], ArkT[:, c, :], True, False,
                                   [X("Vtok"), X("ArkT")], [bank_tok[b]])
                                mm(bk(b)[:, c * 128:(c + 1) * 128], Ubar[:, c, :], RB[:, c, 0, :], False, True,
                                   [X("Qb0"), X("RBa")], [bank_tok[b]])
                            bq = nextbank(6)
                            for c in range(4):
                                mm(bk(bq)[:, c * 128:(c + 1) * 128], KHtok[:, c, :], Vtok[:, c, :], True, False,
                                   [X("KHtok"), X("Vtok")], [bank_tok[bq]])
                                mm(bk(bq)[:, c * 128:(c + 1) * 128], RB[:, c, 1, :], Ubar[:, c, :], False, True,
                                   [X("RBb"), X("Qb0")], [bank_tok[bq]])
                            for h in range(2):
                                cp("act", v3(ObT[HB[h], :]), pb3(b)[HB[h], :, h * 64:(h + 1) * 64], [bank_tok[b]], [X("ObT")])
                            cp("act", Hs[:, :, :], pb3(bq), [bank_tok[bq]], [X("Hs")])
                            yield 0
                            corder = range(4) if d == 0 else range(3, -1, -1)
                            bO = 6 + d
                            for c in corder:
                                mm(bk(bO)[:, c * 128:(c + 1) * 128], Sbf[:, :], RbT[:, c, :], True, True,
                                   [X("Sbf"), X("Qb1")], [bank_tok[bO]])
                                bs = nextbank(6)
                                sreg = bk(bs)[:, 0:128]
                                mm(sreg, GT[:, c, :], Sst[:, :], True, True, [X("GT"), X("Sst")], [bank_tok[bs]])
                                tt("dve", Sst[:, :], sreg, Hs[:, c, :], ALU.add, [bank_tok[bs], X("Hs")], [X("Sst")])
                                cp("act", Sbf[:, :], Sst[:, :], [X("Sst")], [X("Sbf")])
                                yield 0
                            if first:
                                for h in range(2):
                                    tt("dve", v3(o0T[HB[h], tsl]), pb3(bO)[HB[h], :, h * 64:(h + 1) * 64], v3(ObT[HB[h], :]), ALU.add,
                                       [bank_tok[bO], X("ObT")], [T_(f"ost{st}")])
                                SY["out"].add(st)
                            else:
                                while st not in SY["out"]:
                                    yield 0

                                def tmpf(nm):
                                    return B[nm][:, :, :].rearrange("p c k -> p (c k)").bitcast(F32)

                                oo, osq, mean, m2, var = tmpf("Atok"), tmpf("KHtok"), tmpf("Vtok"), tmpf("AakT"), tmpf("ArkT")
                                rstd = QP[1][:, :, 0, :].rearrange("p c k -> p (c k)") if False else None
                                q1 = QP[1][:, :, :, :].rearrange("p c a k -> p (c a k)").bitcast(F32)
                                rstd, xn = q1[:, 0:256], q1[:, 256:512]
                                oo3 = B["Atok"][:, :, :].rearrange("p c k -> p (c k)").bitcast(F32)
                                for h in range(2):
                                    tt("dve", v3(oo3[HB[h], :]), pb3(bO)[HB[h], :, h * 64:(h + 1) * 64], v3(ObT[HB[h], :]), ALU.add,
                                       [bank_tok[bO], X("ObT")], [X("Atok")])
                                tt("dve", oo, oo, o0T[:, tsl], ALU.add, [X("Atok"), T_(f"ost{st}")], [X("Atok")])
                                tt("dve", osq, oo, oo, ALU.mult, [X("Atok")], [X("KHtok")])
                                b = nextbank(6)
                                mm(bk(b)[:, 0:256], bones_f[:, :], oo, True, True, [T_("bones_f"), X("Atok")], [bank_tok[b]])
                                mm(bk(b)[:, 256:512], bones_f[:, :], osq, True, True, [T_("bones_f"), X("KHtok")], [bank_tok[b]])
                                act(mean, bk(b)[:, 0:256], AF.Copy, [bank_tok[b]], [X("Vtok")], scale=1.0 / 64)
                                tt("dve", m2, mean, mean, ALU.mult, [X("Vtok")], [X("AakT")])
                                stt(var, bk(b)[:, 256:512], 1.0 / 64, m2, ALU.mult, ALU.subtract,
                                    [bank_tok[b], X("AakT")], [X("ArkT")])
                                yield 0
                                ts("dve", var, var, 0.0, None, ALU.max, None, [X("ArkT")], [X("ArkT")])
                                act(rstd, var, AF.Ln, [X("ArkT")], [X("QP1")], bias=float(GN_EPS))
                                act(rstd, rstd, AF.Exp, [X("QP1")], [X("QP1")], scale=-0.5)
                                tt("dve", xn, oo, mean, ALU.subtract, [X("Atok"), X("Vtok")], [X("QP1")])
                                tt("dve", xn, xn, rstd, ALU.mult, [X("QP1")], [X("QP1")])
                                ts("dve", xn, xn, pc(l, "gn_w", hp), pc(l, "gn_b", hp), ALU.mult, ALU.add,
                                   [X("QP1"), T_("pcol")], [X("QP1")])
                                tt("dve", xn, xn, B["bon"][q][:, :], ALU.add, [X("QP1"), X("bon")], [X("QP1")])
                                tt("dve", xn, xn, bon0[:, tsl], ALU.add, [X("QP1"), T_(f"bst{st}")], [X("QP1")])
                                tt("dve", yaT[:, hp, tsl], xn, sga[:, tsl], ALU.mult, [X("QP1"), T_("sga")], [T_("yaT")])
                            SY["main"][d] = idx + 1
                            yield 1

                    gens = [prep_gen(0), prep_gen(1), rwkv_gen(0), rwkv_gen(1)]
                    alive = [True] * 4

                    def adv(gi):
                        try:
                            next(gens[gi])
                        except StopIteration:
                            alive[gi] = False

                    guard = 0
                    while any(alive):
                        for gi in range(4):
                            if alive[gi]:
                                adv(gi)
                        guard += 1
                        assert guard < 100000
                chk(8)
                P.barrier()
                for cb_ in range(8):
                    wma = load_w(w_l[:, OFF_MA + cb_ * 128:OFF_MA + (cb_ + 1) * 128], 8)
                    wmb = load_w(w_l[:, OFF_MB + cb_ * 128:OFF_MB + (cb_ + 1) * 128], 8)
                    wpa = load_w(wpa_d[l][:, cb_ * 128:(cb_ + 1) * 128], 4)
                    wpb = load_w(wpb_d[l][:, cb_ * 128:(cb_ + 1) * 128], 4)
                    for t4 in range(T // 512):
                        k = 0
                        tsl = slice(t4 * 512, (t4 + 1) * 512)
                        hsl = slice(1 + t4 * 512, 1 + (t4 + 1) * 512)
                        b = nextbank()
                        zmm(b, 512, wma, 8, lambda kc, hsl=hsl: hT[:, kc, hsl], T_("hT"))
                        act(msg[k][:, 0, :], bk(b)[:, :], AF.Sigmoid, [bank_tok[b]], [T_(f"msg{k}0")])
                        b = nextbank()
                        zmm(b, 512, wmb, 8, lambda kc, hsl=hsl: hT[:, kc, hsl], T_("hT"))
                        act(msg[k][:, 1, :], bk(b)[:, :], AF.Sigmoid, [bank_tok[b]], [T_(f"msg{k}1")])
                        b = nextbank()
                        zmm(b, 512, wpa, 4, lambda kc, tsl=tsl: yaT[:, kc, tsl], T_("yaT"))
                        tt("dve", msg[k][:, 2, :], bk(b)[:, :], msg[k][:, 0, :], ALU.mult, [bank_tok[b], T_(f"msg{k}0")],
                           [T_(f"msg{k}2")])
                        b = nextbank()
                        zmm(b, 512, wpb, 4, lambda kc, tsl=tsl: ybT[:, kc, tsl], T_("ybT"))
                        tt("dve", msg[k][:, 3, :], bk(b)[:, :], msg[k][:, 1, :], ALU.mult, [bank_tok[b], T_(f"msg{k}1")],
                           [T_(f"msg{k}3")])
                        tt("pool", mergedT[:, cb_, tsl], msg[k][:, 2, :], msg[k][:, 3, :], ALU.add,
                           [T_(f"msg{k}2"), T_(f"msg{k}3")], [T_("mergedT")])
                chk(9)
                for kc in range(8):
                    dma("pool", "d_wo", woutT[:, kc, :], wout_d[l][kc * 128:(kc + 1) * 128, :], [], [T_("woutT")])
                src_d = x_d if l == 0 else x1_d
                load_g(1 if l == 0 else 2, 1)
                for i in range(NT):
                    k = i % 2
                    rd = [T_(f"x1d{s}_{i}")] if l == 1 else []
                    dma("sp", f"d_x{k}", xst[k][:, :], src_d[s, i * 128:(i + 1) * 128, :], rd, [T_(f"xst{k}")])
                    for half in range(2):
                        b = nextbank()
                        for kc in range(8):
                            mm(bk(b)[:, :], mergedT[:, kc, i * 128:(i + 1) * 128], woutT[:, kc, half * 512:(half + 1) * 512],
                               kc == 0, kc == 7, [T_("mergedT"), T_("woutT")], [bank_tok[b]])
                        tt("dve", xnew[k][:, half * 512:(half + 1) * 512], bk(b)[:, :], xst[k][:, half * 512:(half + 1) * 512],
                           ALU.add, [bank_tok[b], T_(f"xst{k}")], [T_(f"xnew{k}")])
                    if l == 0:
                        dma("sp", f"d_o{k}", x1_d[s, i * 128:(i + 1) * 128, :], xnew[k][:, :], [T_(f"xnew{k}")], [T_(f"x1d{s}_{i}")])
                        norm_tile(xnew[k][:, :], T_(f"xnew{k}"), i, 1, RMS_EPS)
                    else:
                        norm_tile(xnew[k][:, :], T_(f"xnew{k}"), i, 1, RMS_EPS, out_f=yst[k][:, :], out_tok=T_(f"yst{k}"))
                        dma("sp", f"d_y{k}", y_d[s, i * 128:(i + 1) * 128, :], yst[k][:, :], [T_(f"yst{k}")], [T_(f"yd{s}_{i}")])

    try:
        chk(0)
        body()
    except _Stop:
        pass
    P.final_wait("sp")

    from contextlib import ExitStack
    with ExitStack() as es:
        sem = {}
        for k in list(P.cnt.keys()) + list(P.dcnt.keys()):
            sem[k] = es.enter_context(nc.semaphore(k))
        block = es.enter_context(nc.Block())

        def emit(name, e):
            for waits, fn, inc_key, inc in P.ops[name]:
                for k, v in waits.items():
                    e.wait_ge(sem[k], v)
                if fn is not None:
                    fn(e).then_inc(sem[inc_key], inc)

        @block.tensor
        def _(e):
            emit("pe", e)

        @block.scalar
        def _(e):
            emit("act", e)

        @block.vector
        def _(e):
            emit("dve", e)

        @block.gpsimd
        def _(e):
            emit("pool", e)

        @block.sync
        def _(e):
            emit("sp", e)
    return nc


def _consts():
    c = np.zeros((128, 1280), np.float32)
    p = np.arange(128)
    h = p // 64
    j = p % 64
    same = (h[:, None] == h[None, :])
    jj, ss = j[:, None], j[None, :]
    c[:, 0:128] = same & (jj < ss)
    c[:, 128:256] = same & (jj <= ss)
    c[:, 256:384] = same & (jj > ss)
    c[:, 384:512] = same & (jj >= ss)
    c[:, 512:640] = np.eye(128)
    c[:, 640:768] = same
    rm = np.ones(256, np.float32)
    rm[::64] = 0.0
    c[:, 768:1024] = rm[None, :]
    c[:, 1024:1152] = 1.0
    return c


E_ENTRIES = [(d, d + 1) for d in range(14)] + [(None, 3), (4, 5), (6, 7), (8, 9), (10, None)]


def _etables(rpb):
    Lr = rpb.shape[0]
    kc = np.arange(64)[:, None]
    qc = np.arange(64)[None, :]
    cs = np.clip(qc - 8, 0, 48)
    win = (kc >= cs) & (kc < cs + 16)
    dc = np.clip(kc - qc + 15, 0, 30)
    out = np.zeros((Lr, 4, 128, NE, 128), np.float32)
    mask = np.zeros((128, NE, 128), np.float32)
    for ei, pair in enumerate(E_ENTRIES):
        for half in range(2):
            dr = pair[half]
            if dr is None:
                continue
            for h in range(2):
                mask[half * 64:(half + 1) * 64, ei, h * 64:(h + 1) * 64] = win
                for hp in range(4):
                    out[:, hp, half * 64:(half + 1) * 64, ei, h * 64:(h + 1) * 64] = rpb[:, 2 * hp + h, dr][:, dc]
    return out.reshape(Lr, 4, 128, NE * 128), mask.reshape(128, NE * 128)


def _pack(inp):
    f = lambda a: np.ascontiguousarray(np.asarray(a, dtype=np.float32))
    pcol = np.zeros((128, L * PC_PER_L), np.float32)
    for l in range(L):
        b = l * PC_PER_L
        pcol[:, b:b + 14] = f(inp["shift_mu"])[l].reshape(14, 128).T
        pcol[:, b + 14:b + 22] = f(inp["w0"])[l].reshape(8, 128).T
        pcol[:, b + 22:b + 30] = f(inp["a0"])[l].reshape(8, 128).T
        pcol[:, b + 30:b + 34] = f(inp["k_k"])[l].reshape(4, 128).T
        pcol[:, b + 34:b + 38] = f(inp["k_a"])[l].reshape(4, 128).T
        pcol[:, b + 38:b + 42] = f(inp["r_k"])[l].reshape(4, 128).T
        pcol[:, b + 42:b + 46] = f(inp["gn_w"])[l].reshape(4, 128).T
        pcol[:, b + 46:b + 50] = f(inp["gn_b"])[l].reshape(4, 128).T
    rpbT, emask = _etables(f(inp["rpb"]))
    shared = {
        "w_in": f(inp["w_in"]), "w_pa": f(inp["w_pa"]), "w_pb": f(inp["w_pb"]), "w_out": f(inp["w_out"]),
        "gains": np.concatenate([f(inp["norm_g"]), f(inp["final_g"])[None, :]], 0),
        "pcol": pcol,
        "w2t": f(inp["w2"]).reshape(L, 128, 512), "a2t": f(inp["a2"]).reshape(L, 128, 512),
        "consts": _consts(), "emask": emask, "rpbT": rpbT,
    }
    return shared


_NC_CACHE = {}


def run(x_all, inp, T, NSEQ, ncores=NCORES):
    key = (T, NSEQ)
    if key not in _NC_CACHE:
        _NC_CACHE[key] = build_program(T, NSEQ)
    nc = _NC_CACHE[key]
    shared = _pack(inp)
    in_maps = []
    for c in range(ncores):
        m = dict(shared)
        m["x"] = np.ascontiguousarray(x_all[c * NSEQ:(c + 1) * NSEQ])
        in_maps.append(m)
    res = run_bass_kernel_spmd(nc, in_maps, core_ids=list(range(ncores)))
    return np.concatenate([r["y"] for r in res.results], 0), res


def kernel(x_prompt, x_sample, norm_g, w_in, shift_mu, w0, w2, a0, a2, k_k, k_a, r_k,
           gn_w, gn_b, rpb, w_pa, w_pb, w_out, final_g):
    inp = dict(norm_g=norm_g, w_in=w_in, shift_mu=shift_mu, w0=w0, w2=w2, a0=a0, a2=a2, k_k=k_k, k_a=k_a,
               r_k=r_k, gn_w=gn_w, gn_b=gn_b, rpb=rpb, w_pa=w_pa, w_pb=w_pb, w_out=w_out, final_g=final_g)
    xp = np.asarray(x_prompt, np.float32)
    xs = np.asarray(x_sample, np.float32)
    x_all = np.concatenate([xp, xs], 0)
    y, _ = run(x_all, inp, 2048, 3)
    return y[:xp.shape[0]].astype(np.float32), y[xp.shape[0]:].astype(np.float32)
```
